# Optimizing a Trainium2 kernel written in Bass

```python
import math
import jax, jax.numpy as jnp
from jax import lax
import numpy as np

D_MODEL = 1024
BATCH = 4
SEQ = 8192
DEPTH = 1

MLA_HEADS = 8
MLA_NOPE = 64
MLA_ROPE = 32
MLA_V = 64
Q_LORA = 256
KV_LORA = 128
D_MLA = MLA_HEADS * MLA_V
MLA_SCALE = 1.0 / math.sqrt(MLA_NOPE + MLA_ROPE)
ROPE_BASE = 10000.0

SWA_HEADS = 8
SWA_KV_HEADS = 2
SWA_HEAD_DIM = 64
SWA_GROUP = SWA_HEADS // SWA_KV_HEADS
D_SWA = SWA_HEADS * SWA_HEAD_DIM
WINDOW = 128
BLOCK = 128
SWA_SCALE = 1.0 / math.sqrt(SWA_HEAD_DIM)

N_BUCKETS = 32
MAX_DISTANCE = 128

D_MIX = D_MLA + D_SWA

ALPHA = (2.0 * DEPTH) ** 0.25
BETA = (8.0 * DEPTH) ** -0.25

IN_SPLITS = (Q_LORA, KV_LORA, MLA_ROPE, D_MLA,
             SWA_HEADS * SWA_HEAD_DIM, SWA_KV_HEADS * SWA_HEAD_DIM,
             SWA_KV_HEADS * SWA_HEAD_DIM, D_SWA)
D_IN = int(sum(IN_SPLITS))
IN_OFFSETS = tuple(int(o) for o in np.cumsum(IN_SPLITS)[:-1])

kernel_name = "hybrid_mla_swa_deepnorm_block"


def _rms_norm(x, g, eps=1e-6):
    xf = x.astype(jnp.float32)
    y = xf * lax.rsqrt(jnp.mean(xf * xf, axis=-1, keepdims=True) + eps)
    return y.astype(x.dtype) * g


def _layer_norm(x, g, b, eps=1e-5):
    xf = x.astype(jnp.float32)
    mu = jnp.mean(xf, axis=-1, keepdims=True)
    var = jnp.mean(jnp.square(xf - mu), axis=-1, keepdims=True)
    return ((xf - mu) * lax.rsqrt(var + eps)).astype(x.dtype) * g + b


def _rope(x, cos, sin):
    half = x.shape[-1] // 2
    x1, x2 = x[..., :half], x[..., half:]
    cos = cos.astype(x.dtype)
    sin = sin.astype(x.dtype)
    return jnp.concatenate([x1 * cos - x2 * sin, x2 * cos + x1 * sin], axis=-1)


def _t5_bucket(rel):
    half = N_BUCKETS // 2
    ret = np.where(rel > 0, half, 0)
    n = np.abs(rel)
    max_exact = half // 2
    large = max_exact + (np.log(np.maximum(n, 1).astype(np.float32) / max_exact)
                         / np.log(MAX_DISTANCE / max_exact) * (half - max_exact)).astype(np.int32)
    large = np.minimum(large, half - 1)
    return (ret + np.where(n < max_exact, n, large)).astype(np.int32)


def _mla(c_q, c_kv, k_rope, g_q, g_kv, w_uq, w_ukv, cos, sin):
    B, S, _ = c_q.shape
    nb = S // BLOCK
    q = (_rms_norm(c_q, g_q) @ w_uq).reshape(B, S, MLA_HEADS, MLA_NOPE + MLA_ROPE)
    q_nope = q[..., :MLA_NOPE]
    q_rope = _rope(q[..., MLA_NOPE:], cos[:, None, :], sin[:, None, :])
    kv = (_rms_norm(c_kv, g_kv) @ w_ukv).reshape(B, S, MLA_HEADS, MLA_NOPE + MLA_V)
    k_nope, v = kv[..., :MLA_NOPE], kv[..., MLA_NOPE:]
    k_r = _rope(k_rope, cos, sin)
    qn = q_nope.reshape(B, nb, BLOCK, MLA_HEADS, MLA_NOPE).transpose(1, 0, 2, 3, 4)
    qr = q_rope.reshape(B, nb, BLOCK, MLA_HEADS, MLA_ROPE).transpose(1, 0, 2, 3, 4)

    def attend_block(args):
        qn_b, qr_b = args
        s = (jnp.einsum('bqhd,bkhd->bhqk', qn_b, k_nope)
             + jnp.einsum('bqhr,bkr->bhqk', qr_b, k_r)).astype(jnp.float32) * MLA_SCALE
        p = jax.nn.softmax(s, axis=-1).astype(v.dtype)
        return jnp.einsum('bhqk,bkhd->bqhd', p, v)

    o = lax.map(attend_block, (qn, qr))
    return o.transpose(1, 0, 2, 3, 4).reshape(B, S, D_MLA)


def _swa(q, k, v, sink, rel_bias):
    B, S, _, _ = q.shape
    nb = S // BLOCK
    qb = q.reshape(B, nb, BLOCK, SWA_KV_HEADS, SWA_GROUP, SWA_HEAD_DIM)
    pad = ((0, 0), (BLOCK, BLOCK), (0, 0), (0, 0))
    kp = jnp.pad(k, pad).reshape(B, nb + 2, BLOCK, SWA_KV_HEADS, SWA_HEAD_DIM)
    vp = jnp.pad(v, pad).reshape(B, nb + 2, BLOCK, SWA_KV_HEADS, SWA_HEAD_DIM)
    kband = jnp.concatenate([kp[:, :-2], kp[:, 1:-1], kp[:, 2:]], axis=2)
    vband = jnp.concatenate([vp[:, :-2], vp[:, 1:-1], vp[:, 2:]], axis=2)

    q_loc = np.arange(BLOCK)
    k_loc = np.arange(3 * BLOCK) - BLOCK
    rel = k_loc[None, :] - q_loc[:, None]
    band = np.abs(rel) <= WINDOW
    bucket = _t5_bucket(rel)
    bias = rel_bias[bucket].astype(jnp.float32)
    bias = bias.transpose(2, 0, 1).reshape(SWA_KV_HEADS, SWA_GROUP, BLOCK, 3 * BLOCK)

    k_abs = jnp.arange(nb)[:, None] * BLOCK + jnp.asarray(k_loc)[None, :]
    valid = (k_abs >= 0) & (k_abs < S)
    mask = jnp.asarray(band)[None, :, :] & valid[:, None, :]

    s = jnp.einsum('bnqhgd,bnjhd->bnhgqj', qb, kband).astype(jnp.float32) * SWA_SCALE
    s = jnp.where(mask[None, :, None, None], s + bias, -jnp.inf)
    sink_b = sink.astype(jnp.float32).reshape(SWA_KV_HEADS, SWA_GROUP)[None, None, :, :, None, None]
    m = jnp.maximum(jnp.max(s, axis=-1, keepdims=True), sink_b)
    e = jnp.exp(s - m)
    p = e / (jnp.sum(e, axis=-1, keepdims=True) + jnp.exp(sink_b - m))
    o = jnp.einsum('bnhgqj,bnjhd->bnqhgd', p.astype(v.dtype), vband)
    return o.reshape(B, S, D_SWA)


def setup_inputs(seed: int = 0) -> dict:
    key = jax.random.key(seed)
    ks = jax.random.split(key, 12)
    f32 = jnp.float32
    x = jax.random.normal(ks[0], (BATCH, SEQ, D_MODEL), f32)
    w_in = jax.random.normal(ks[1], (D_MODEL, D_IN), f32) * D_MODEL ** -0.5
    g_q = 1.0 + 0.05 * jax.random.normal(ks[2], (Q_LORA,), f32)
    g_kv = 1.0 + 0.05 * jax.random.normal(ks[3], (KV_LORA,), f32)
    w_uq = jax.random.normal(ks[4], (Q_LORA, MLA_HEADS * (MLA_NOPE + MLA_ROPE)), f32) * Q_LORA ** -0.5
    w_ukv = jax.random.normal(ks[5], (KV_LORA, MLA_HEADS * (MLA_NOPE + MLA_V)), f32) * KV_LORA ** -0.5
    sink = 0.5 * jax.random.normal(ks[6], (SWA_HEADS,), f32)
    rel_bias = 0.5 * jax.random.normal(ks[7], (N_BUCKETS, SWA_HEADS), f32)
    w_out = jax.random.normal(ks[8], (D_MIX, D_MODEL), f32) * (D_MIX ** -0.5) * BETA
    ln_g = 1.0 + 0.05 * jax.random.normal(ks[9], (D_MODEL,), f32)
    ln_b = 0.02 * jax.random.normal(ks[10], (D_MODEL,), f32)
    return {"x": x, "w_in": w_in, "g_q": g_q, "g_kv": g_kv, "w_uq": w_uq,
            "w_ukv": w_ukv, "sink": sink, "rel_bias": rel_bias, "w_out": w_out,
            "ln_g": ln_g, "ln_b": ln_b}


def reference(x, w_in, g_q, g_kv, w_uq, w_ukv, sink, rel_bias, w_out, ln_g, ln_b):
    B, S, _ = x.shape
    pos = jnp.arange(S, dtype=jnp.float32)
    inv_freq = ROPE_BASE ** (-jnp.arange(0, MLA_ROPE, 2, dtype=jnp.float32) / MLA_ROPE)
    ang = pos[:, None] * inv_freq[None, :]
    cos, sin = jnp.cos(ang), jnp.sin(ang)

    h = x
    for _layer in range(DEPTH):
        proj = h @ w_in
        c_q, c_kv, k_rope, gate_a, q_s, k_s, v_s, gate_b = jnp.split(proj, IN_OFFSETS, axis=-1)
        o_a = _mla(c_q, c_kv, k_rope, g_q, g_kv, w_uq, w_ukv, cos, sin)
        o_b = _swa(q_s.reshape(B, S, SWA_HEADS, SWA_HEAD_DIM),
                   k_s.reshape(B, S, SWA_KV_HEADS, SWA_HEAD_DIM),
                   v_s.reshape(B, S, SWA_KV_HEADS, SWA_HEAD_DIM),
                   sink, rel_bias)
        mixed = jnp.concatenate([o_a * jax.nn.silu(gate_a), o_b * jax.nn.silu(gate_b)], axis=-1)
        h = _layer_norm(ALPHA * h + mixed @ w_out, ln_g, ln_b)
    return h
```

```python
import math
from contextlib import ExitStack

import numpy as np
import concourse.bass as bass
import concourse.mybir as mybir
from concourse.bass_utils import run_bass_kernel_spmd

F32 = mybir.dt.float32
BF16 = mybir.dt.bfloat16
AF = mybir.ActivationFunctionType
ALU = mybir.AluOpType

D_MODEL = 1024
BATCH = 4
SEQ = 8192
NQ = 4096
NK = 8192
MLA_SCALE = 1.0 / math.sqrt(96.0)
SWA_SCALE = 0.125
ALPHA = 2.0 ** 0.25
N_CORES = 8
NEG = -30000.0

WA_COLS = 1024
WS_COLS = 1408


class Buf:
    __slots__ = ("name", "w", "r")

    def __init__(self, name):
        self.name = name
        self.w = None
        self.r = []


class Op:
    __slots__ = ("eng", "fn", "deps", "users", "is_dma", "sem", "val", "qidx")


class Prog:
    ENGS = ("tensor", "vector", "scalar", "gpsimd", "sync")
    DMA_POOL = 8

    def __init__(self):
        self.ops = {e: [] for e in self.ENGS}
        self.ndma = {e: 0 for e in self.ENGS}
        self.dma_last = {}
        self.last = {e: None for e in self.ENGS}
        self.enabled = True

    def op(self, eng, fn, reads=(), writes=(), deps=(), dma=False, force=()):
        if not self.enabled:
            return None
        o = Op()
        o.eng, o.fn, o.is_dma, o.users = eng, fn, dma, False
        o.sem = o.val = o.qidx = None
        d = set(x for x in deps if x is not None)
        for b in tuple(reads) + tuple(writes):
            if b.w is not None:
                d.add(b.w)
        for b in writes:
            d.update(b.r)
        d.discard(o)
        if dma:
            o.qidx = self.ndma[eng]
            self.ndma[eng] += 1
            prev = self.dma_last.get((eng, o.qidx % self.DMA_POOL))
            if prev is not None:
                d.add(prev)
            self.dma_last[(eng, o.qidx % self.DMA_POOL)] = o
            o.users = True
        o.deps = [x for x in d if not (x.eng == "tensor" and eng == "tensor" and not x.is_dma)]
        o.deps += [x for x in force if x is not None and x not in o.deps]
        for x in o.deps:
            x.users = True
        for b in reads:
            b.r.append(o)
        for b in writes:
            b.w = o
            b.r = []
        self.ops[eng].append(o)
        if fn is not None:
            self.last[eng] = o
        return o

    def barrier(self):
        lasts = [self.last[e] for e in self.ENGS if self.last[e] is not None]
        lasts += list(self.dma_last.values())
        for e in self.ENGS:
            self.op(e, None, deps=lasts)

    def finalize_tokens(self, sems, dma_sems):
        for e in self.ENGS:
            cnt = 0
            for o in self.ops[e]:
                if o.is_dma:
                    o.sem = dma_sems[e][o.qidx % self.DMA_POOL]
                    o.val = 16 * (o.qidx // self.DMA_POOL + 1)
                elif o.users:
                    cnt += 1
                    o.sem = sems[e]
                    o.val = cnt

    def emit(self, eng_name, e):
        waited = {}
        for o in self.ops[eng_name]:
            need = {}
            for x in o.deps:
                k = id(x.sem)
                if k not in need or need[k][1] < x.val:
                    need[k] = (x.sem, x.val)
            for k, (sem, val) in need.items():
                if waited.get(k, 0) >= val:
                    continue
                e.wait_ge(sem, val)
                waited[k] = val
            if o.fn is None:
                continue
            inst = o.fn(e)
            if o.is_dma:
                inst.then_inc(o.sem, 16)
            elif o.users:
                inst.then_inc(o.sem, 1)


def _t5_bucket(rel):
    half = 16
    ret = np.where(rel > 0, half, 0)
    n = np.abs(rel)
    max_exact = half // 2
    large = max_exact + (np.log(np.maximum(n, 1).astype(np.float32) / max_exact)
                         / np.log(128 / max_exact) * (half - max_exact)).astype(np.int32)
    large = np.minimum(large, half - 1)
    return (ret + np.where(n < max_exact, n, large)).astype(np.int32)


def _rope_tables(positions):
    inv_freq = 10000.0 ** (-np.arange(0, 32, 2, dtype=np.float64) / 32.0)
    ang = positions.astype(np.float64)[:, None] * inv_freq[None, :]
    cos = np.cos(ang).astype(np.float32)
    sin = np.sin(ang).astype(np.float32)
    cos2 = np.concatenate([cos, cos], axis=1)
    sin2 = np.concatenate([-sin, sin], axis=1)
    tab = np.concatenate([cos2, sin2], axis=1)
    ng = positions.shape[0] // 512
    return np.ascontiguousarray(tab.reshape(ng, 512, 64).transpose(0, 2, 1))


def build_program(phases=("S1", "S2", "A", "C", "D")):
    nc = bass.Bass("TRN2", target_bir_lowering=False)
    dt = nc.dram_tensor
    xq = dt("xq", [NQ, D_MODEL], F32, kind="ExternalInput").ap()
    xo = dt("xo", [NQ, D_MODEL], F32, kind="ExternalInput").ap()
    xh = dt("xh", [256, D_MODEL], F32, kind="ExternalInput").ap()
    hv = dt("hv", [128, 2], F32, kind="ExternalInput").ap()
    wA = [dt(f"wA{c}", [128, WA_COLS], F32, kind="ExternalInput").ap() for c in range(8)]
    wS = [dt(f"wS{c}", [128, WS_COLS], F32, kind="ExternalInput").ap() for c in range(8)]
    wuq = [dt(f"wuq{c}", [128, 1024], F32, kind="ExternalInput").ap() for c in range(2)]
    wukv = [dt("wukv", [128, 1024], F32, kind="ExternalInput").ap()]
    wout = [dt(f"wout{c}", [128, 1024], F32, kind="ExternalInput").ap() for c in range(8)]
    gq = dt("gq", [128, 2], F32, kind="ExternalInput").ap()
    gkv = dt("gkv", [128, 1], F32, kind="ExternalInput").ap()
    cs = dt("cs", [16, 64, 512], F32, kind="ExternalInput").ap()
    biasT = [dt(f"biasT{kv}", [128, 1536], F32, kind="ExternalInput").ap() for kv in range(2)]
    sink = dt("sink", [8], F32, kind="ExternalInput").ap()
    lng = dt("lng", [1024], F32, kind="ExternalInput").ap()
    lnb = dt("lnb", [1024], F32, kind="ExternalInput").ap()
    ident = dt("ident", [128, 128], F32, kind="ExternalInput").ap()
    y = dt("y", [NQ, D_MODEL], F32, kind="ExternalOutput").ap()

    P = Prog()
    cnt = {"ev": 0}

    def mm(out, lhsT, rhs, start, stop, reads, writes, force=()):
        return P.op("tensor", lambda e: e.matmul(out, lhsT=lhsT, rhs=rhs, start=start, stop=stop),
                    reads, writes, force=force)

    def tr(out, in_, idn, reads, writes):
        return P.op("tensor", lambda e: e.transpose(out=out, in_=in_, identity=idn), reads, writes)

    def act(out, in_, func, reads, writes, scale=None, bias=None, accum_out=None):
        kw = {}
        if scale is not None:
            kw["scale"] = scale
        if bias is not None:
            kw["bias"] = bias
        if accum_out is not None:
            kw["accum_out"] = accum_out
        return P.op("scalar", lambda e: e.activation(out=out, in_=in_, func=func, **kw), reads, writes)

    def vcopy(eng, out, in_, reads, writes):
        if eng == "scalar":
            return P.op("scalar", lambda e: e.copy(out=out, in_=in_), reads, writes)
        return P.op(eng, lambda e: e.tensor_copy(out=out, in_=in_), reads, writes)

    def evac(out, in_, reads, writes):
        cnt["ev"] += 1
        return vcopy("vector" if cnt["ev"] % 2 else "scalar", out, in_, reads, writes)

    def tt(eng, out, in0, in1, op, reads, writes):
        return P.op(eng, lambda e: e.tensor_tensor(out=out, in0=in0, in1=in1, op=op), reads, writes)

    def stt(out, in0, scalar, in1, op0, op1, reads, writes):
        return P.op("vector", lambda e: e.scalar_tensor_tensor(out=out, in0=in0, scalar=scalar, in1=in1,
                                                                op0=op0, op1=op1), reads, writes)

    def ts(eng, out, in0, s1, s2, op0, op1, reads, writes):
        if op1 is None:
            return P.op(eng, lambda e: e.tensor_scalar(out=out, in0=in0, scalar1=s1, scalar2=None, op0=op0),
                        reads, writes)
        return P.op(eng, lambda e: e.tensor_scalar(out=out, in0=in0, scalar1=s1, scalar2=s2, op0=op0, op1=op1),
                    reads, writes)

    def recip(out, in_, reads, writes):
        return P.op("vector", lambda e: e.reciprocal(out=out, in_=in_), reads, writes)

    def memset(eng, ap, val, writes):
        return P.op(eng, lambda e: e.memset(ap, val), (), writes)

    def dma(eng, out, in_, reads, writes):
        return P.op(eng, lambda e: e.dma_start(out=out, in_=in_), reads, writes, dma=True)

    with ExitStack() as top:
        def T(es, name, shape, dtype):
            return es.enter_context(nc.sbuf_tensor(name, shape, dtype))

        ps = top.enter_context(nc.psum_tensor("ps", [128, 8, 512], F32))
        psb = ps.bitcast(BF16)
        PB = [Buf(f"pb{i}") for i in range(8)]

        mixedT = T(top, "mixedT", [128, 8, NQ], BF16)
        MX = [[Buf(f"mx{c}_{g}") for g in range(8)] for c in range(8)]
        identf = T(top, "identf", [128, 128], F32)
        identb = T(top, "identb", [128, 128], BF16)
        ones32 = T(top, "ones32", [128, 128], F32)
        zeros32 = T(top, "zeros32", [128, 128], F32)
        gq_t = T(top, "gq_t", [128, 2], F32)
        gkv_t = T(top, "gkv_t", [128, 1], F32)
        hv_t = T(top, "hv_t", [128, 2], F32)
        sink_t = T(top, "sink_t", [128, 8], F32)
        CONST = Buf("const")
        stg = {}

        def alloc_staging(es, tag, full=True):
            stg["xstage"] = [T(es, f"xstage{tag}{i}", [128, 1024], F32) for i in range(3)]
            stg["XS"] = [Buf(f"xs{i}") for i in range(3)]
            stg["wstage"] = [T(es, f"wstage{tag}{i}", [128, WS_COLS], F32) for i in range(2)]
            stg["WSTG"] = [Buf(f"wstg{i}") for i in range(2)]
            if full:
                stg["xbt"] = [T(es, f"xbt{tag}{i}", [128, 1024], BF16) for i in range(4)]
                stg["XB"] = [Buf(f"xb{i}") for i in range(4)]
                stg["xT"] = [T(es, f"xT{tag}{i}", [128, 8, 512], BF16) for i in range(2)]
                stg["XT"] = [[Buf(f"xt{i}_{t}") for t in range(4)] for i in range(2)]

        dma("sync", identf[:], ident[:, :], (), [CONST])
        dma("sync", gq_t[:], gq[:, :], (), [CONST])
        dma("sync", gkv_t[:], gkv[:, :], (), [CONST])
        dma("sync", hv_t[:], hv[:, :], (), [CONST])
        dma("sync", sink_t[:], sink.partition_broadcast(128), (), [CONST])
        vcopy("vector", identb[:], identf[:], [CONST], [CONST])
        memset("vector", ones32[:], 1.0, [CONST])
        memset("vector", zeros32[:], 0.0, [CONST])
        eps_t = T(top, "eps_t", [128, 1], F32)
        memset("vector", eps_t[:], 1e-6, [CONST])

        xcount = {"n": 0, "g": 0}

        def load_weight_chunks(src, ncols, dst_fn, nchunks, WB):
            wstage, WSTG = stg["wstage"], stg["WSTG"]
            for c in range(nchunks):
                s = c % 2
                dma("sync", wstage[s][:, 0:ncols], src[c][:, :], (), [WSTG[s]])
                vcopy("scalar" if c % 2 == 0 else "vector", dst_fn(c), wstage[s][:, 0:ncols], [WSTG[s]], [WB])

        def x_cast(srcs):
            xstage, XS, xbt, XB = (stg[k] for k in ("xstage", "XS", "xbt", "XB"))
            ids = []
            for src in srcs:
                i = xcount["n"]
                xcount["n"] += 1
                s3, s4 = i % 3, i % 4
                dma("sync", xstage[s3][:], src, (), [XS[s3]])
                vcopy("vector" if i % 2 else "scalar", xbt[s4][:], xstage[s3][:], [XS[s3]], [XB[s4]])
                ids.append(i)
            return ids

        def x_tr(ids, gs):
            xbt, XB, xT, XT = (stg[k] for k in ("xbt", "XB", "xT", "XT"))
            for t, i in enumerate(ids):
                s2, s4 = i % 2, i % 4
                bank = s2
                for c in range(8):
                    tr(psb[:, bank, c * 128:(c + 1) * 128], xbt[s4][:, c * 128:(c + 1) * 128], identb[:],
                       [XB[s4], CONST], [PB[bank]])
                evac(xT[gs][:, :, t * 128:(t + 1) * 128],
                     psb[:, bank, 0:1024].rearrange("p (c k) -> p c k", c=8),
                     [PB[bank]], [XT[gs][t]])

        with ExitStack() as sS:
            QT = T(sS, "QT", [128, 4, NQ], BF16)
            QTB = [[Buf(f"qt{c}_{g}") for g in range(8)] for c in range(4)]
            KT = T(sS, "KT", [128, 2, 34 * 128], BF16)
            KTB = [Buf(f"kt{g}") for g in range(9)]
            VA = T(sS, "VA", [128, 34, 2, 128], BF16)
            VAB = [Buf(f"va{lk}") for lk in range(34)]
            for lk in range(34):
                memset("gpsimd", VA[:, lk, :, 64:128], 1.0, [VAB[lk]])

            with ExitStack() as s1:
                P.enabled = "S1" in phases
                alloc_staging(s1, "s")
                xT, XT = stg["xT"], stg["XT"]
                wSb = T(s1, "wSb", [128, 8, WS_COLS], BF16)
                WSB = Buf("wsb")
                load_weight_chunks(wS, WS_COLS, lambda c: wSb[:, c, :], 8, WSB)

                xids = {}

                def s1_xc(gi):
                    if gi > 8:
                        return
                    if gi == 0:
                        srcs = [xh[0:128, :], xh[128:256, :]]
                    else:
                        g = gi - 1
                        srcs = [xq[(g * 4 + t) * 128:(g * 4 + t + 1) * 128, :] for t in range(4)]
                    xids[gi] = x_cast(srcs)

                def s1_x(gi):
                    if gi > 8:
                        return
                    x_tr(xids[gi], gi % 2)

                def s1_group(gi):
                    gs = gi % 2
                    if gi == 0:
                        lks = [0, 33]
                    else:
                        g = gi - 1
                        lks = [1 + g * 4 + t for t in range(4)]
                    nt = len(lks)
                    Tn = nt * 128
                    xr = XT[gs][0:nt]
                    s1_xc(gi + 1)
                    pj = [2, 3, 4]
                    pjc = {"i": 0}

                    def proj(c0, M=128):
                        b = pj[pjc["i"] % 3]
                        pjc["i"] += 1
                        for dc in range(8):
                            mm(ps[0:M, b, 0:Tn], wSb[:, dc, c0:c0 + M], xT[gs][:, dc, 0:Tn],
                               dc == 0, dc == 7, xr + [WSB], [PB[b]])
                        return b

                    for kv in range(2):
                        b = proj(512 + kv * 128)
                        if gi == 0:
                            evac(KT[:, kv, 0:128], ps[:, b, 0:128], [PB[b]], [KTB[0]])
                            evac(KT[:, kv, 33 * 128:34 * 128], ps[:, b, 128:256], [PB[b]], [KTB[0]])
                        else:
                            evac(KT[:, kv, lks[0] * 128:lks[0] * 128 + 512], ps[:, b, 0:512], [PB[b]], [KTB[gi]])
                    for t in range(nt):
                        b = 5 + (t % 2)
                        for dc in range(8):
                            mm(ps[:, b, 0:128], xT[gs][:, dc, t * 128:(t + 1) * 128], wSb[:, dc, 768:896],
                               dc == 0, dc == 7, [XT[gs][t], WSB], [PB[b]])
                        vcopy("vector", VA[:, lks[t], :, 0:64],
                              ps[:, b, 0:128].rearrange("p (k d) -> p k d", k=2), [PB[b]], [VAB[lks[t]]])
                    if gi == 0:
                        for j, lk in enumerate((0, 33)):
                            ts("vector", VA[:, lk, :, :], VA[:, lk, :, :], hv_t[:, j:j + 1], None, ALU.mult, None,
                               [CONST, VAB[lk]], [VAB[lk]])
                        s1_x(gi + 1)
                        return
                    g = gi - 1
                    for c in range(4):
                        b = proj(c * 128)
                        evac(QT[:, c, g * 512:(g + 1) * 512], ps[:, b, 0:512], [PB[b]], [QTB[c][g]])
                    s1_x(gi + 1)
                    for c in range(4):
                        b = proj(896 + c * 128)
                        act(mixedT[:, 4 + c, g * 512:(g + 1) * 512], ps[:, b, 0:512], AF.Silu,
                            [PB[b]], [MX[4 + c][g]])

                s1_xc(0)
                s1_x(0)
                for gi in range(9):
                    s1_group(gi)
            P.enabled = True
            P.barrier()

            with ExitStack() as s2:
                P.enabled = "S2" in phases
                bias_t = T(s2, "bias_t", [128, 2, 3, 512], F32)
                esb = T(s2, "esb", [128, 2, 512], F32)
                tmpS = [T(s2, f"tmpS{i}", [128, 3, 512], F32) for i in range(2)]
                TS_ = [Buf(f"tmps{i}") for i in range(2)]
                PTs = [T(s2, f"PTs{i}", [128, 3, 512], BF16) for i in range(2)]
                PTB = [Buf(f"pts{i}") for i in range(2)]
                den = [T(s2, f"den{i}", [128, 512], F32) for i in range(2)]
                DEN = [Buf(f"den{i}") for i in range(2)]
                tmpo = [T(s2, f"tmpo{i}", [128, 512], F32) for i in range(2)]
                TMO = [Buf(f"tmo{i}") for i in range(2)]
                C2 = Buf("c2")

                def gpos(g):
                    return (g % 2) * 2 + g // 2
                ind16 = T(s2, "ind16", [1, 128], BF16)
                esb16 = T(s2, "esb16", [1, 2, 512], BF16)
                memset("vector", ind16[0:1, 0:64], 0.0, [C2])
                memset("vector", ind16[0:1, 64:128], 1.0, [C2])
                for kv in range(2):
                    dma("sync", bias_t[:, kv, :, :].rearrange("p b c -> p (b c)"), biasT[kv][:, :], (), [C2])
                for kv in range(2):
                    for g in range(4):
                        h = kv * 4 + g
                        act(esb[:, kv, gpos(g) * 128:(gpos(g) + 1) * 128], zeros32[:], AF.Exp, [CONST], [C2],
                            bias=sink_t[:, h:h + 1])
                vcopy("vector", esb16[:], esb[0:1, :, :], [C2], [C2])

                def s2_qk(it):
                    n, kv = it // 2, it % 2
                    sl = it % 2
                    kbufs = sorted(set(0 if lk in (0, 33) else 1 + (lk - 1) // 4 for lk in (n, n + 1, n + 2)))
                    lastmm = None
                    for hf in range(2):
                        for kb in range(3):
                            b = sl * 3 + kb
                            m = mm(ps[:, b, hf * 256:(hf + 1) * 256],
                                   KT[hf * 64:(hf + 1) * 64, kv, (n + kb) * 128:(n + kb + 1) * 128],
                                   QT[hf * 64:(hf + 1) * 64, kv * 2:kv * 2 + 2, n * 128:(n + 1) * 128],
                                   True, True,
                                   [KTB[k] for k in kbufs] + [QTB[kv * 2][n // 4], QTB[kv * 2 + 1][n // 4]], [PB[b]],
                                   force=[lastmm] if (hf == 1 and kb == 0) else [])
                            if hf == 0:
                                lastmm = m

                def s2_bias(it):
                    n, kv = it // 2, it % 2
                    sl = it % 2
                    stt(tmpS[sl][:].rearrange("p a b -> p (a b)"),
                        ps[:, sl * 3:sl * 3 + 3, :].rearrange("p a b -> p (a b)"), SWA_SCALE,
                        bias_t[:, kv, :, :].rearrange("p a b -> p (a b)"),
                        ALU.mult, ALU.add, [PB[sl * 3 + kb] for kb in range(3)] + [C2], [TS_[sl]])

                def s2_exp(it):
                    sl = it % 2
                    act(PTs[sl][:], tmpS[sl][:], AF.Exp, [TS_[sl]], [PTB[sl]])

                def s2_pv(it):
                    n, kv = it // 2, it % 2
                    sl = it % 2
                    ob = 6 + sl
                    for kb in range(3):
                        mm(ps[:, ob, :], VA[:, n + kb, kv, :], PTs[sl][:, kb, :], kb == 0, False,
                           [VAB[n + kb], PTB[sl]], [PB[ob]])
                    mm(ps[:, ob, :], ind16[0:1, :], esb16[0:1, kv, :], False, True, [C2], [PB[ob]])

                def s2_rden(it):
                    sl = it % 2
                    ob = 6 + sl
                    act(den[sl][64:128, :], ps[64:128, ob, :], AF.Ln, [PB[ob]], [DEN[sl]])
                    act(den[sl][64:128, :], den[sl][64:128, :], AF.Exp, [DEN[sl]], [DEN[sl]], scale=-1.0)

                def s2_rest(it):
                    n, kv = it // 2, it % 2
                    sl = it % 2
                    ob = 6 + sl
                    for hf in range(2):
                        rows = slice(hf * 64, (hf + 1) * 64)
                        gc = slice(hf * 256, (hf + 1) * 256)
                        tt("vector", tmpo[sl][rows, gc], ps[0:64, ob, gc], den[sl][64:128, gc], ALU.mult,
                           [PB[ob], DEN[sl]], [TMO[sl]])
                    for g in range(4):
                        h = kv * 4 + g
                        c, hf = 4 + h // 2, h % 2
                        rows = slice(hf * 64, (hf + 1) * 64)
                        dst = mixedT[rows, c, n * 128:(n + 1) * 128]
                        tt("gpsimd", dst, tmpo[sl][rows, gpos(g) * 128:(gpos(g) + 1) * 128], dst, ALU.mult,
                           [TMO[sl], MX[c][n // 4]], [MX[c][n // 4]])

                s2_qk(0)
                s2_bias(0)
                s2_exp(0)
                for it in range(64):
                    if it + 1 < 64:
                        s2_qk(it + 1)
                    if it >= 1:
                        s2_rden(it - 1)
                    if it + 1 < 64:
                        s2_bias(it + 1)
                    if it >= 1:
                        s2_rest(it - 1)
                    if it + 1 < 64:
                        s2_exp(it + 1)
                    s2_pv(it)
                s2_rden(63)
                s2_rest(63)
            P.enabled = True
            P.barrier()

        with ExitStack() as sM:
            cqn = T(sM, "cqn", [128, 2, NQ], BF16)
            CQN = [Buf(f"cqn{g}") for g in range(8)]
            ckvn = T(sM, "ckvn", [128, NK], BF16)
            CKV = [Buf(f"ckv{g}") for g in range(16)]
            KTm = T(sM, "KTm", [128, NK], BF16)
            KNB = [Buf(f"knb{g}") for g in range(16)]
            KRB = [Buf(f"krb{g}") for g in range(16)]
            wuqb = T(sM, "wuqb", [128, 2, 1024], BF16)
            wukvb = T(sM, "wukvb", [128, 1024], BF16)
            WQB = Buf("wqb")
            WKB = Buf("wkb")
            cst = [T(sM, f"cst{i}", [128, 512], F32) for i in range(2)]
            CST = [Buf(f"cst{i}") for i in range(2)]
            t12 = [T(sM, f"t12_{i}", [128, 512], F32) for i in range(2)]
            T12 = [Buf(f"t12_{i}") for i in range(2)]
            csn = {"n": 0}

            def rope(b, s, dst, DST, ct=None, CB=None):
                ct = cst[s] if ct is None else ct
                CB = CST[s] if CB is None else CB
                tt("vector", t12[s][96:128, :], ps[96:128, b, :], ct[96:128, :], ALU.mult,
                   [PB[b], CB], [T12[s]])
                tt("vector", ps[64:96, b, :], ps[64:96, b, :], ct[64:96, :], ALU.mult,
                   [PB[b], CB], [PB[b]])
                tt("vector", dst, ps[64:96, b, :], t12[s][96:128, :], ALU.add, [PB[b], T12[s]], [DST])

            with ExitStack() as sa:
                P.enabled = "A" in phases
                alloc_staging(sa, "a")
                xT, XT = stg["xT"], stg["XT"]
                wAb = T(sa, "wAb", [128, 8, WA_COLS], BF16)
                WAB = Buf("wab")
                sq = T(sa, "sq", [128, 3, 512], F32)
                SQ = [Buf(f"sq{i}") for i in range(3)]
                rb = T(sa, "rb", [128, 2, 512], F32)
                RB = Buf("rb")
                load_weight_chunks(wA, WA_COLS, lambda c: wAb[:, c, :], 8, WAB)
                load_weight_chunks(wuq, 1024, lambda c: wuqb[:, c, :], 2, WQB)
                load_weight_chunks(wukv, 1024, lambda c: wukvb[:, :], 1, WKB)

                xids = {}

                def a_xc(kg):
                    if kg > 15:
                        return
                    src = xq if kg < 8 else xo
                    g = kg % 8
                    srcs = [src[(g * 4 + t) * 128:(g * 4 + t + 1) * 128, :] for t in range(4)]
                    xids[kg] = x_cast(srcs)

                def a_x(kg):
                    if kg > 15:
                        return
                    x_tr(xids[kg], kg % 2)

                def a_group(kg):
                    own = kg < 8
                    gs = kg % 2
                    g = kg % 8
                    xr = XT[gs][0:4]
                    a_xc(kg + 1)

                    def proj(b, c0, M=128):
                        for dc in range(8):
                            mm(ps[0:M, b, :], wAb[:, dc, c0:c0 + M], xT[gs][:, dc, :],
                               dc == 0, dc == 7, xr + [WAB], [PB[b]])

                    proj(4, 256)
                    act(sq[:, 2, :], ps[:, 4, :], AF.Square, [PB[4]], [SQ[2]])
                    mm(ps[:, 6, :], ones32[:], sq[:, 2, :], True, True, [CONST, SQ[2]], [PB[6]])
                    act(rb[:, 1, :], ps[:, 6, :], AF.Ln, [PB[6]], [RB], scale=1.0 / 128.0, bias=eps_t[:, 0:1])
                    act(rb[:, 1, :], rb[:, 1, :], AF.Exp, [RB], [RB], scale=-0.5)
                    if own:
                        proj(2, 0)
                        proj(3, 128)
                        act(sq[:, 0, :], ps[:, 2, :], AF.Square, [PB[2]], [SQ[0]])
                        act(sq[:, 1, :], ps[:, 3, :], AF.Square, [PB[3]], [SQ[1]])
                        mm(ps[:, 5, :], ones32[:], sq[:, 0, :], True, False, [CONST, SQ[0]], [PB[5]])
                        mm(ps[:, 5, :], ones32[:], sq[:, 1, :], False, True, [CONST, SQ[1]], [PB[5]])
                        act(rb[:, 0, :], ps[:, 5, :], AF.Ln, [PB[5]], [RB], scale=1.0 / 256.0, bias=eps_t[:, 0:1])
                        act(rb[:, 0, :], rb[:, 0, :], AF.Exp, [RB], [RB], scale=-0.5)
                        for cc in range(2):
                            stt(cqn[:, cc, g * 512:(g + 1) * 512], ps[:, 2 + cc, :], gq_t[:, cc:cc + 1], rb[:, 0, :],
                                ALU.mult, ALU.mult, [PB[2 + cc], RB, CONST], [CQN[g]])
                    stt(ckvn[:, kg * 512:(kg + 1) * 512], ps[:, 4, :], gkv_t[:, 0:1], rb[:, 1, :],
                        ALU.mult, ALU.mult, [PB[4], RB, CONST], [CKV[kg]])
                    proj(7, 384)
                    s = csn["n"] % 2
                    csn["n"] += 1
                    dma("sync", cst[s][64:128, :], cs[kg, :, :], (), [CST[s]])
                    rope(7, s, KTm[64:96, kg * 512:(kg + 1) * 512], KRB[kg])
                    a_x(kg + 1)
                    if own:
                        for c in range(4):
                            b = (5, 6, 7)[c % 3]
                            proj(b, 512 + c * 128)
                            act(mixedT[:, c, g * 512:(g + 1) * 512], ps[:, b, :], AF.Silu, [PB[b]], [MX[c][g]])

                a_xc(0)
                a_x(0)
                for kg in range(16):
                    a_group(kg)
            P.enabled = True
            P.barrier()

            with ExitStack() as sc:
                P.enabled = "C" in phases
                VAm = T(sc, "VAm", [128, 64, 128], BF16)
                VMB = [Buf(f"vmb{i}") for i in range(8)]
                QTm = T(sc, "QTm", [128, NQ], BF16)
                QMB = [Buf(f"qmb{i}") for i in range(8)]
                PT = [T(sc, f"PT{i}", [128, 3, 512], BF16) for i in range(3)]
                PTB = [Buf(f"ptb{i}") for i in range(3)]
                rden = [T(sc, f"rden{i}", [128, 512], F32) for i in range(2)]
                RDN = [Buf(f"rdn{i}") for i in range(2)]
                tmpo = [T(sc, f"tmpoC{i}", [128, 512], F32) for i in range(2)]
                TMO = [Buf(f"tmoC{i}") for i in range(2)]

                def vo(h):
                    return (0, 64) if h % 2 == 0 else (64, 0)

                def pcopy(out, in_, reads, writes, up):
                    if up:
                        evac(out, in_, reads, writes)
                    else:
                        vcopy("vector", out, in_, reads, writes)

                def prep_k(h, kg, b, up=False):
                    mm(ps[0:64, b, :], wukvb[:, h * 128:h * 128 + 64], ckvn[:, kg * 512:(kg + 1) * 512],
                       True, True, [WKB, CKV[kg]], [PB[b]])
                    pcopy(KTm[0:64, kg * 512:(kg + 1) * 512], ps[0:64, b, :], [PB[b]], [KNB[kg]], up)

                def prep_v(h, vb, b, up=False):
                    voff, ooff = vo(h)
                    for j in range(8):
                        kblk = vb * 8 + j
                        mm(ps[:, b, j * 64:(j + 1) * 64], ckvn[:, kblk * 128:(kblk + 1) * 128],
                           wukvb[:, h * 128 + 64:h * 128 + 128], True, True,
                           [WKB, CKV[kblk // 4]], [PB[b]])
                    memset("gpsimd", VAm[:, vb * 8:(vb + 1) * 8, ooff:ooff + 64], 1.0, [VMB[vb]])
                    pcopy(VAm[:, vb * 8:(vb + 1) * 8, voff:voff + 64],
                          ps[:, b, :].rearrange("p (j d) -> p j d", j=8), [PB[b]], [VMB[vb]], up)

                cstq = [T(sc, f"cstq{i}", [128, 512], F32) for i in range(8)]
                CSTQ = [Buf(f"cstq{i}") for i in range(8)]

                def cs_dma(h, qg, b=None):
                    dma("sync", cstq[qg][64:128, :], cs[qg, :, :], (), [CSTQ[qg]])

                def prep_q(h, qg, b, up=False):
                    for cc in range(2):
                        mm(ps[:, b, :], wuqb[:, cc, h * 128:(h + 1) * 128], cqn[:, cc, qg * 512:(qg + 1) * 512],
                           cc == 0, cc == 1, [WQB, CQN[qg]], [PB[b]])
                    pcopy(QTm[0:64, qg * 512:(qg + 1) * 512], ps[0:64, b, :], [PB[b]], [QMB[qg]], up)
                    rope(b, (h * 8 + qg) % 2, QTm[64:96, qg * 512:(qg + 1) * 512], QMB[qg], cstq[qg], CSTQ[qg])

                for qg in range(8):
                    cs_dma(0, qg)
                    prep_q(0, qg, qg % 6, True)
                for kg in range(16):
                    prep_k(0, kg, kg % 6, True)
                for vb in range(8):
                    prep_v(0, vb, vb % 6, True)

                hooks = {}
                now_hooks = {}
                for h in range(7):
                    for qg in range(8):
                        now_hooks.setdefault((h, 6, 4 * qg), []).append((cs_dma, h + 1, qg))
                        hooks.setdefault((h, 7, 0) if qg < 7 else (h, 7, 63), []).append((prep_q, h + 1, qg))
                    for kg in range(16):
                        hooks.setdefault((h, 7, 4 * kg + 3), []).append((prep_k, h + 1, kg))
                    for vb in range(8):
                        hooks.setdefault((h, 7, 8 * vb + 7), []).append((prep_v, h + 1, vb))
                pending = []
                tcnt = {"n": 0}

                def fin_part(h, qg, c):
                    voff, ooff = vo(h)
                    s, ob = qg % 2, 6 + qg % 2
                    orow = slice(voff, voff + 64)
                    srow = slice(ooff, ooff + 64)
                    cc_ = slice(c * 128, (c + 1) * 128)
                    recip(rden[s][srow, cc_], ps[srow, ob, cc_], [PB[ob]], [RDN[s]])
                    tt("vector", tmpo[s][orow, cc_], ps[orow, ob, cc_], rden[s][srow, cc_], ALU.mult,
                       [PB[ob], RDN[s]], [TMO[s]])
                    dst = mixedT[orow, h // 2, qg * 512 + c * 128:qg * 512 + (c + 1) * 128]
                    tt("gpsimd", dst, tmpo[s][orow, cc_], dst, ALU.mult,
                       [TMO[s], MX[h // 2][qg]], [MX[h // 2][qg]])

                tiles = [(h, qg, kb) for h in range(8) for qg in range(8) for kb in range(64)]
                glist = []
                tpos = {"i": 0}
                last_prep = {"j": -100}

                def prep_would_fire(groups):
                    pq = [it_[0] for it_ in pending]
                    n_ = tcnt["n"]
                    for g_ in groups:
                        for (h, qg, kb) in g_:
                            if kb == 63:
                                pq += [fin_part] * 4
                            for fn, hh, idx in hooks.get((h, qg, kb), ()):
                                pq.append(fn)
                            n_ += 1
                            if pq and n_ % 3 == 0:
                                if pq.pop(0) is not fin_part:
                                    return True
                    return False

                def form_group(j):
                    if tpos["i"] >= len(tiles):
                        return False
                    assert j == len(glist)
                    n_ = 3
                    if j % 2 == 1:
                        prev = [glist[x] for x in (j - 2, j - 1) if x >= 0]
                        if last_prep["j"] > j - 6 or prep_would_fire(prev):
                            n_ = 2
                    glist.append(tiles[tpos["i"]:tpos["i"] + n_])
                    tpos["i"] += n_
                    return True

                def emit_qk(j):
                    sb0 = 0 if j % 2 == 0 else 3
                    for i, (h, qg, kb) in enumerate(glist[j]):
                        mm(ps[:, sb0 + i, :], KTm[0:96, kb * 128:(kb + 1) * 128],
                           QTm[0:96, qg * 512:(qg + 1) * 512], True, True,
                           [KNB[kb // 4], KRB[kb // 4], QMB[qg]], [PB[sb0 + i]])

                def emit_exp(j):
                    sb0 = 0 if j % 2 == 0 else 3
                    pslot = j % 3
                    ng = len(glist[j])
                    act(PT[pslot][:, 0:ng, :], ps[:, sb0:sb0 + ng, :], AF.Exp,
                        [PB[sb0 + i] for i in range(ng)], [PTB[pslot]], scale=MLA_SCALE)

                def emit_pv(j):
                    pslot = j % 3
                    gt = glist[j]
                    for i, (h, qg, kb) in enumerate(gt):
                        voff, ooff = vo(h)
                        ob = 6 + qg % 2
                        mm(ps[:, ob, :], VAm[:, kb, :], PT[pslot][:, i, :], kb == 0, kb == 63,
                           [VMB[kb // 8], PTB[pslot]], [PB[ob]])
                        for fn, hh, idx in now_hooks.get((h, qg, kb), ()):
                            fn(hh, idx)
                        if kb == 63:
                            for c in range(4):
                                pending.append((fin_part, h, qg, c))
                        for fn, hh, idx in hooks.get((h, qg, kb), ()):
                            pending.append((fn, hh, idx, 5))
                        tcnt["n"] += 1
                        if pending and tcnt["n"] % 3 == 0:
                            it_ = pending.pop(0)
                            if it_[0] is not fin_part:
                                last_prep["j"] = j
                            it_[0](*it_[1:])

                form_group(0)
                emit_qk(0)
                form_group(1)
                emit_qk(1)
                j = 0
                while j < len(glist):
                    emit_exp(j)
                    if form_group(j + 2):
                        emit_qk(j + 2)
                    emit_pv(j)
                    j += 1
                while pending:
                    it_ = pending.pop(0)
                    it_[0](*it_[1:])
            P.enabled = True
            P.barrier()

        with ExitStack() as sd:
            alloc_staging(sd, "d", full=False)
            xstage, XS = stg["xstage"], stg["XS"]
            woutb = T(sd, "woutb", [128, 8, 1024], BF16)
            WOB = Buf("wob")
            lng_t = T(sd, "lng_t", [128, 1024], F32)
            lnb_t = T(sd, "lnb_t", [128, 1024], F32)
            LNC = Buf("lnc")
            NS = 4
            junk = T(sd, "junk", [128, 1024], BF16)
            JNK = Buf("junk")
            yb = [T(sd, f"yb{i}", [128, 1024], F32) for i in range(NS)]
            YB = [Buf(f"yb{i}") for i in range(NS)]
            st6 = [T(sd, f"st6_{i}", [128, 12], F32) for i in range(NS)]
            ST6 = [Buf(f"st6_{i}") for i in range(NS)]
            mv = [T(sd, f"mv{i}", [128, 2], F32) for i in range(NS)]
            MV = [Buf(f"mv{i}") for i in range(NS)]
            rstd = [T(sd, f"rstd{i}", [128, 1], F32) for i in range(NS)]
            RSD = [Buf(f"rsd{i}") for i in range(NS)]
            nmr = [T(sd, f"nmr{i}", [128, 1], F32) for i in range(NS)]
            NMR = [Buf(f"nmr{i}") for i in range(NS)]
            load_weight_chunks(wout, 1024, lambda c: woutb[:, c, :], 8, WOB)
            dma("sync", lng_t[:], lng.partition_broadcast(128), (), [LNC])
            dma("sync", lnb_t[:], lnb.partition_broadcast(128), (), [LNC])
            stores = []
            NT = NQ // 128

            def load_x(t):
                if t < NT:
                    s3 = t % 3
                    dma("sync", xstage[s3][:], xq[t * 128:(t + 1) * 128, :], (), [XS[s3]])

            def d_a(t):
                s, s3 = t % NS, t % 3
                for hh in range(2):
                    b = (t % 2) * 2 + hh
                    for k in range(8):
                        mm(ps[:, b, :], mixedT[:, k, t * 128:(t + 1) * 128], woutb[:, k, hh * 512:(hh + 1) * 512],
                           k == 0, k == 7, [MX[k][t // 4], WOB], [PB[b]])
                    stt(yb[s][:, hh * 512:(hh + 1) * 512], xstage[s3][:, hh * 512:(hh + 1) * 512], ALPHA,
                        ps[:, b, :], ALU.mult, ALU.add, [XS[s3], PB[b]], [YB[s]])
                act(junk[:], yb[s][:], AF.Identity, [YB[s]], [JNK, ST6[s]], accum_out=st6[s][:, 0:1])
                act(junk[:], yb[s][:], AF.Square, [YB[s]], [JNK, ST6[s]], accum_out=st6[s][:, 1:2])

            def d_a2(t):
                s = t % NS
                ts("vector", mv[s][:, 0:1], st6[s][:, 0:1], 1.0 / D_MODEL, None, ALU.mult, None, [ST6[s]], [MV[s]])
                tt("vector", st6[s][:, 2:3], mv[s][:, 0:1], mv[s][:, 0:1], ALU.mult, [MV[s]], [ST6[s]])
                stt(mv[s][:, 1:2], st6[s][:, 1:2], 1.0 / D_MODEL, st6[s][:, 2:3], ALU.mult, ALU.subtract,
                    [ST6[s]], [MV[s]])
                act(rstd[s][:], mv[s][:, 1:2], AF.Sqrt, [MV[s]], [RSD[s]], bias=1e-5)

            def d_b1(t):
                s = t % NS
                recip(rstd[s][:], rstd[s][:], [RSD[s]], [RSD[s]])
                ts("vector", nmr[s][:], mv[s][:, 0:1], rstd[s][:, 0:1], -1.0, ALU.mult, ALU.mult,
                   [MV[s], RSD[s]], [NMR[s]])
                act(yb[s][:], yb[s][:], AF.Identity, [YB[s], RSD[s], NMR[s]], [YB[s]],
                    scale=rstd[s][:, 0:1], bias=nmr[s][:, 0:1])

            def d_b2(t):
                s = t % NS
                tt("vector", yb[s][:], yb[s][:], lng_t[:], ALU.mult, [YB[s], LNC], [YB[s]])
                tt("gpsimd", yb[s][:], yb[s][:], lnb_t[:], ALU.add, [YB[s], LNC], [YB[s]])
                stores.append(dma("sync", y[t * 128:(t + 1) * 128, :], yb[s][:], [YB[s]], []))

            load_x(0)
            load_x(1)
            load_x(2)
            d_a(0)
            d_a2(0)
            d_a(1)
            d_a2(1)
            d_b1(0)
            for t in range(NT):
                load_x(t + 3)
                if t + 2 < NT:
                    d_a(t + 2)
                d_b2(t)
                if t + 1 < NT:
                    d_b1(t + 1)
                if t + 2 < NT:
                    d_a2(t + 2)
            P.op("sync", None, deps=stores)
            P.op("scalar", None, deps=stores)

            with ExitStack() as se:
                sems = {e: se.enter_context(nc.semaphore(f"s_{e}")) for e in Prog.ENGS}
                dma_sems = {e: [se.enter_context(nc.semaphore(f"d_{e}{i}")) for i in range(Prog.DMA_POOL)]
                            for e in Prog.ENGS}
                P.finalize_tokens(sems, dma_sems)
                block = se.enter_context(nc.Block())

                @block.sync
                def _(e):
                    P.emit("sync", e)

                @block.tensor
                def _(e):
                    P.emit("tensor", e)

                @block.vector
                def _(e):
                    P.emit("vector", e)

                @block.scalar
                def _(e):
                    P.emit("scalar", e)

                @block.gpsimd
                def _(e):
                    P.emit("gpsimd", e)
    return nc


_CACHE = {}


def kernel(x, w_in, g_q, g_kv, w_uq, w_ukv, sink, rel_bias, w_out, ln_g, ln_b):
    x = np.asarray(x, np.float32)
    w_in = np.asarray(w_in, np.float32)
    w_uq = np.asarray(w_uq, np.float32)
    w_ukv = np.asarray(w_ukv, np.float32)
    w_out = np.asarray(w_out, np.float32)
    rel_bias = np.asarray(rel_bias, np.float32)
    o_cq, o_ckv, o_kr, o_ga, o_qs, o_ks, o_vs, o_gb = 0, 256, 384, 416, 928, 1440, 1568, 1696
    kr_cols = np.arange(o_kr, o_kr + 32)
    kr_sw = np.concatenate([kr_cols[16:], kr_cols[:16]])
    colsA = np.concatenate([np.arange(o_cq, o_cq + 256), np.arange(o_ckv, o_ckv + 128), kr_cols, kr_sw,
                            kr_cols, kr_sw, np.arange(o_ga, o_ga + 512)])
    k0 = np.arange(o_ks, o_ks + 64)
    k1 = np.arange(o_ks + 64, o_ks + 128)
    colsS = np.concatenate([np.arange(o_qs, o_qs + 512), k0, k0, k1, k1, np.arange(o_vs, o_vs + 128),
                            np.arange(o_gb, o_gb + 512)])
    wA = np.ascontiguousarray(w_in[:, colsA])
    wS = np.ascontiguousarray(w_in[:, colsS])
    cq = []
    for h in range(8):
        base = h * 96
        rope = np.arange(base + 64, base + 96)
        cq.append(np.concatenate([np.arange(base, base + 64), rope, rope[16:], rope[:16]]))
    wuq = np.ascontiguousarray(w_uq[:, np.concatenate(cq)])
    gq = np.ascontiguousarray(np.asarray(g_q, np.float32).reshape(2, 128).T)
    gkv = np.ascontiguousarray(np.asarray(g_kv, np.float32).reshape(128, 1))
    q_loc = np.arange(128)
    k_loc = np.arange(384) - 128
    rel = k_loc[None, :] - q_loc[:, None]
    band = np.abs(rel) <= 128
    bucket = _t5_bucket(rel)
    bias = rel_bias[bucket]
    bias = np.where(band[:, :, None], bias, np.float32(NEG)).astype(np.float32)
    bT = bias.reshape(128, 3, 128, 2, 4).transpose(2, 3, 1, 4, 0)
    bT = bT[:, :, :, [0, 2, 1, 3], :]
    biasT = np.ascontiguousarray(bT.reshape(128, 3072))
    ident = np.eye(128, dtype=np.float32)

    in_maps = []
    for c in range(N_CORES):
        b, half = c // 2, c % 2
        own0 = half * NQ
        oth0 = (1 - half) * NQ
        xq_c = x[b, own0:own0 + NQ]
        xo_c = x[b, oth0:oth0 + NQ]
        xh_c = np.zeros((256, D_MODEL), np.float32)
        hv_c = np.zeros((128, 2), np.float32)
        if own0 - 128 >= 0:
            xh_c[0:128] = x[b, own0 - 128:own0]
            hv_c[:, 0] = 1.0
        if own0 + NQ + 128 <= SEQ:
            xh_c[128:256] = x[b, own0 + NQ:own0 + NQ + 128]
            hv_c[:, 1] = 1.0
        pos = np.concatenate([np.arange(own0, own0 + NQ), np.arange(oth0, oth0 + NQ)])
        in_maps.append({
            "xq": np.ascontiguousarray(xq_c), "xo": np.ascontiguousarray(xo_c), "xh": xh_c, "hv": hv_c,
            "wukv": w_ukv, "gq": gq, "gkv": gkv,
            "cs": _rope_tables(pos), "sink": np.asarray(sink, np.float32),
            **{f"wA{i}": wA[i * 128:(i + 1) * 128] for i in range(8)},
            **{f"wS{i}": wS[i * 128:(i + 1) * 128] for i in range(8)},
            **{f"wuq{i}": wuq[i * 128:(i + 1) * 128] for i in range(2)},
            **{f"wout{i}": w_out[i * 128:(i + 1) * 128] for i in range(8)},
            **{f"biasT{kv}": np.ascontiguousarray(biasT[:, kv * 1536:(kv + 1) * 1536]) for kv in range(2)},
            "lng": np.asarray(ln_g, np.float32), "lnb": np.asarray(ln_b, np.float32), "ident": ident,
        })
    if "nc" not in _CACHE:
        _CACHE["nc"] = build_program()
    res = run_bass_kernel_spmd(_CACHE["nc"], in_maps, core_ids=list(range(N_CORES)))
    out = np.empty((BATCH, SEQ, D_MODEL), np.float32)
    for c in range(N_CORES):
        b, half = c // 2, c % 2
        out[b, half * NQ:(half + 1) * NQ] = res.results[c]["y"]
    return out
```

```python
import math
from contextlib import ExitStack

import numpy as np
import concourse.bass as bass
import concourse.mybir as mybir
from concourse.bass_utils import run_bass_kernel_spmd

F32 = mybir.dt.float32
BF16 = mybir.dt.bfloat16
AF = mybir.ActivationFunctionType
ALU = mybir.AluOpType

D_MODEL = 1024
BATCH = 4
SEQ = 8192
NQ = 4096
NK = 8192
MLA_SCALE = 1.0 / math.sqrt(96.0)
SWA_SCALE = 0.125
ALPHA = 2.0 ** 0.25
N_CORES = 8
NEG = -30000.0

WA_COLS = 1024
WS_COLS = 1408


class Buf:
    __slots__ = ("name", "w", "r")

    def __init__(self, name):
        self.name = name
        self.w = None
        self.r = []


class Op:
    __slots__ = ("eng", "fn", "deps", "users", "is_dma", "sem", "val", "qidx")


class Prog:
    ENGS = ("tensor", "vector", "scalar", "gpsimd", "sync")
    DMA_POOL = 8

    def __init__(self):
        self.ops = {e: [] for e in self.ENGS}
        self.ndma = {e: 0 for e in self.ENGS}
        self.dma_last = {}
        self.last = {e: None for e in self.ENGS}
        self.enabled = True

    def op(self, eng, fn, reads=(), writes=(), deps=(), dma=False, force=()):
        if not self.enabled:
            return None
        o = Op()
        o.eng, o.fn, o.is_dma, o.users = eng, fn, dma, False
        o.sem = o.val = o.qidx = None
        d = set(x for x in deps if x is not None)
        for b in tuple(reads) + tuple(writes):
            if b.w is not None:
                d.add(b.w)
        for b in writes:
            d.update(b.r)
        d.discard(o)
        if dma:
            o.qidx = self.ndma[eng]
            self.ndma[eng] += 1
            prev = self.dma_last.get((eng, o.qidx % self.DMA_POOL))
            if prev is not None:
                d.add(prev)
            self.dma_last[(eng, o.qidx % self.DMA_POOL)] = o
            o.users = True
        o.deps = [x for x in d if not (x.eng == "tensor" and eng == "tensor" and not x.is_dma)]
        o.deps += [x for x in force if x is not None and x not in o.deps]
        for x in o.deps:
            x.users = True
        for b in reads:
            b.r.append(o)
        for b in writes:
            b.w = o
            b.r = []
        self.ops[eng].append(o)
        if fn is not None:
            self.last[eng] = o
        return o

    def barrier(self):
        lasts = [self.last[e] for e in self.ENGS if self.last[e] is not None]
        lasts += list(self.dma_last.values())
        for e in self.ENGS:
            self.op(e, None, deps=lasts)

    def finalize_tokens(self, sems, dma_sems):
        for e in self.ENGS:
            cnt = 0
            for o in self.ops[e]:
                if o.is_dma:
                    o.sem = dma_sems[e][o.qidx % self.DMA_POOL]
                    o.val = 16 * (o.qidx // self.DMA_POOL + 1)
                elif o.users:
                    cnt += 1
                    o.sem = sems[e]
                    o.val = cnt

    def emit(self, eng_name, e):
        waited = {}
        for o in self.ops[eng_name]:
            need = {}
            for x in o.deps:
                k = id(x.sem)
                if k not in need or need[k][1] < x.val:
                    need[k] = (x.sem, x.val)
            for k, (sem, val) in need.items():
                if waited.get(k, 0) >= val:
                    continue
                e.wait_ge(sem, val)
                waited[k] = val
            if o.fn is None:
                continue
            inst = o.fn(e)
            if o.is_dma:
                inst.then_inc(o.sem, 16)
            elif o.users:
                inst.then_inc(o.sem, 1)


def _t5_bucket(rel):
    half = 16
    ret = np.where(rel > 0, half, 0)
    n = np.abs(rel)
    max_exact = half // 2
    large = max_exact + (np.log(np.maximum(n, 1).astype(np.float32) / max_exact)
                         / np.log(128 / max_exact) * (half - max_exact)).astype(np.int32)
    large = np.minimum(large, half - 1)
    return (ret + np.where(n < max_exact, n, large)).astype(np.int32)


def _rope_tables(positions):
    inv_freq = 10000.0 ** (-np.arange(0, 32, 2, dtype=np.float64) / 32.0)
    ang = positions.astype(np.float64)[:, None] * inv_freq[None, :]
    cos = np.cos(ang).astype(np.float32)
    sin = np.sin(ang).astype(np.float32)
    cos2 = np.concatenate([cos, cos], axis=1)
    sin2 = np.concatenate([-sin, sin], axis=1)
    tab = np.concatenate([cos2, sin2], axis=1)
    ng = positions.shape[0] // 512
    return np.ascontiguousarray(tab.reshape(ng, 512, 64).transpose(0, 2, 1))


def build_program(phases=("S1", "S2", "A", "C", "D")):
    nc = bass.Bass("TRN2", target_bir_lowering=False)
    dt = nc.dram_tensor
    xq = dt("xq", [NQ, D_MODEL], F32, kind="ExternalInput").ap()
    xo = dt("xo", [NQ, D_MODEL], F32, kind="ExternalInput").ap()
    xh = dt("xh", [256, D_MODEL], F32, kind="ExternalInput").ap()
    hv = dt("hv", [128, 2], F32, kind="ExternalInput").ap()
    wA = [dt(f"wA{c}", [128, WA_COLS], F32, kind="ExternalInput").ap() for c in range(8)]
    wS = [dt(f"wS{c}", [128, WS_COLS], F32, kind="ExternalInput").ap() for c in range(8)]
    wuq = [dt(f"wuq{c}", [128, 1024], F32, kind="ExternalInput").ap() for c in range(2)]
    wukv = [dt("wukv", [128, 1024], F32, kind="ExternalInput").ap()]
    wout = [dt(f"wout{c}", [128, 1024], F32, kind="ExternalInput").ap() for c in range(8)]
    gq = dt("gq", [128, 2], F32, kind="ExternalInput").ap()
    gkv = dt("gkv", [128, 1], F32, kind="ExternalInput").ap()
    cs = dt("cs", [16, 64, 512], F32, kind="ExternalInput").ap()
    biasT = [dt(f"biasT{kv}", [128, 1536], F32, kind="ExternalInput").ap() for kv in range(2)]
    sink = dt("sink", [8], F32, kind="ExternalInput").ap()
    lng = dt("lng", [1024], F32, kind="ExternalInput").ap()
    lnb = dt("lnb", [1024], F32, kind="ExternalInput").ap()
    ident = dt("ident", [128, 128], F32, kind="ExternalInput").ap()
    y = dt("y", [NQ, D_MODEL], F32, kind="ExternalOutput").ap()

    P = Prog()
    cnt = {"ev": 0}

    def mm(out, lhsT, rhs, start, stop, reads, writes, force=()):
        return P.op("tensor", lambda e: e.matmul(out, lhsT=lhsT, rhs=rhs, start=start, stop=stop),
                    reads, writes, force=force)

    def tr(out, in_, idn, reads, writes):
        return P.op("tensor", lambda e: e.transpose(out=out, in_=in_, identity=idn), reads, writes)

    def act(out, in_, func, reads, writes, scale=None, bias=None, accum_out=None):
        kw = {}
        if scale is not None:
            kw["scale"] = scale
        if bias is not None:
            kw["bias"] = bias
        if accum_out is not None:
            kw["accum_out"] = accum_out
        return P.op("scalar", lambda e: e.activation(out=out, in_=in_, func=func, **kw), reads, writes)

    def vcopy(eng, out, in_, reads, writes):
        if eng == "scalar":
            return P.op("scalar", lambda e: e.copy(out=out, in_=in_), reads, writes)
        return P.op(eng, lambda e: e.tensor_copy(out=out, in_=in_), reads, writes)

    def evac(out, in_, reads, writes):
        cnt["ev"] += 1
        return vcopy("vector" if cnt["ev"] % 2 else "scalar", out, in_, reads, writes)

    def tt(eng, out, in0, in1, op, reads, writes):
        return P.op(eng, lambda e: e.tensor_tensor(out=out, in0=in0, in1=in1, op=op), reads, writes)

    def stt(out, in0, scalar, in1, op0, op1, reads, writes):
        return P.op("vector", lambda e: e.scalar_tensor_tensor(out=out, in0=in0, scalar=scalar, in1=in1,
                                                                op0=op0, op1=op1), reads, writes)

    def ts(eng, out, in0, s1, s2, op0, op1, reads, writes):
        if op1 is None:
            return P.op(eng, lambda e: e.tensor_scalar(out=out, in0=in0, scalar1=s1, scalar2=None, op0=op0),
                        reads, writes)
        return P.op(eng, lambda e: e.tensor_scalar(out=out, in0=in0, scalar1=s1, scalar2=s2, op0=op0, op1=op1),
                    reads, writes)

    def recip(out, in_, reads, writes):
        return P.op("vector", lambda e: e.reciprocal(out=out, in_=in_), reads, writes)

    def memset(eng, ap, val, writes):
        return P.op(eng, lambda e: e.memset(ap, val), (), writes)

    def dma(eng, out, in_, reads, writes):
        return P.op(eng, lambda e: e.dma_start(out=out, in_=in_), reads, writes, dma=True)

    with ExitStack() as top:
        def T(es, name, shape, dtype):
            return es.enter_context(nc.sbuf_tensor(name, shape, dtype))

        ps = top.enter_context(nc.psum_tensor("ps", [128, 8, 512], F32))
        psb = ps.bitcast(BF16)
        PB = [Buf(f"pb{i}") for i in range(8)]

        mixedT = T(top, "mixedT", [128, 8, NQ], BF16)
        MX = [[Buf(f"mx{c}_{g}") for g in range(8)] for c in range(8)]
        identf = T(top, "identf", [128, 128], F32)
        identb = T(top, "identb", [128, 128], BF16)
        ones32 = T(top, "ones32", [128, 128], F32)
        zeros32 = T(top, "zeros32", [128, 128], F32)
        gq_t = T(top, "gq_t", [128, 2], F32)
        gkv_t = T(top, "gkv_t", [128, 1], F32)
        hv_t = T(top, "hv_t", [128, 2], F32)
        sink_t = T(top, "sink_t", [128, 8], F32)
        CONST = Buf("const")
        stg = {}

        def alloc_staging(es, tag, full=True):
            stg["xstage"] = [T(es, f"xstage{tag}{i}", [128, 1024], F32) for i in range(3)]
            stg["XS"] = [Buf(f"xs{i}") for i in range(3)]
            stg["wstage"] = [T(es, f"wstage{tag}{i}", [128, WS_COLS], F32) for i in range(2)]
            stg["WSTG"] = [Buf(f"wstg{i}") for i in range(2)]
            if full:
                stg["xbt"] = [T(es, f"xbt{tag}{i}", [128, 1024], BF16) for i in range(4)]
                stg["XB"] = [Buf(f"xb{i}") for i in range(4)]
                stg["xT"] = [T(es, f"xT{tag}{i}", [128, 8, 512], BF16) for i in range(2)]
                stg["XT"] = [[Buf(f"xt{i}_{t}") for t in range(4)] for i in range(2)]

        dma("sync", identf[:], ident[:, :], (), [CONST])
        dma("sync", gq_t[:], gq[:, :], (), [CONST])
        dma("sync", gkv_t[:], gkv[:, :], (), [CONST])
        dma("sync", hv_t[:], hv[:, :], (), [CONST])
        dma("sync", sink_t[:], sink.partition_broadcast(128), (), [CONST])
        vcopy("vector", identb[:], identf[:], [CONST], [CONST])
        memset("vector", ones32[:], 1.0, [CONST])
        memset("vector", zeros32[:], 0.0, [CONST])
        eps_t = T(top, "eps_t", [128, 1], F32)
        memset("vector", eps_t[:], 1e-6, [CONST])

        xcount = {"n": 0, "g": 0}

        def load_weight_chunks(src, ncols, dst_fn, nchunks, WB):
            wstage, WSTG = stg["wstage"], stg["WSTG"]
            for c in range(nchunks):
                s = c % 2
                dma("sync", wstage[s][:, 0:ncols], src[c][:, :], (), [WSTG[s]])
                vcopy("scalar" if c % 2 == 0 else "vector", dst_fn(c), wstage[s][:, 0:ncols], [WSTG[s]], [WB])

        def x_cast(srcs):
            xstage, XS, xbt, XB = (stg[k] for k in ("xstage", "XS", "xbt", "XB"))
            ids = []
            for src in srcs:
                i = xcount["n"]
                xcount["n"] += 1
                s3, s4 = i % 3, i % 4
                dma("sync", xstage[s3][:], src, (), [XS[s3]])
                vcopy("vector" if i % 2 else "scalar", xbt[s4][:], xstage[s3][:], [XS[s3]], [XB[s4]])
                ids.append(i)
            return ids

        def x_tr(ids, gs):
            xbt, XB, xT, XT = (stg[k] for k in ("xbt", "XB", "xT", "XT"))
            for t, i in enumerate(ids):
                s2, s4 = i % 2, i % 4
                bank = s2
                for c in range(8):
                    tr(psb[:, bank, c * 128:(c + 1) * 128], xbt[s4][:, c * 128:(c + 1) * 128], identb[:],
                       [XB[s4], CONST], [PB[bank]])
                evac(xT[gs][:, :, t * 128:(t + 1) * 128],
                     psb[:, bank, 0:1024].rearrange("p (c k) -> p c k", c=8),
                     [PB[bank]], [XT[gs][t]])

        with ExitStack() as sS:
            QT = T(sS, "QT", [128, 4, NQ], BF16)
            QTB = [[Buf(f"qt{c}_{g}") for g in range(8)] for c in range(4)]
            KT = T(sS, "KT", [128, 2, 34 * 128], BF16)
            KTB = [Buf(f"kt{g}") for g in range(9)]
            VA = T(sS, "VA", [128, 34, 2, 128], BF16)
            VAB = [Buf(f"va{lk}") for lk in range(34)]
            for lk in range(34):
                memset("gpsimd", VA[:, lk, :, 64:128], 1.0, [VAB[lk]])

            with ExitStack() as s1:
                P.enabled = "S1" in phases
                alloc_staging(s1, "s")
                xT, XT = stg["xT"], stg["XT"]
                wSb = T(s1, "wSb", [128, 8, WS_COLS], BF16)
                WSB = Buf("wsb")
                load_weight_chunks(wS, WS_COLS, lambda c: wSb[:, c, :], 8, WSB)

                xids = {}

                def s1_xc(gi):
                    if gi > 8:
                        return
                    if gi == 0:
                        srcs = [xh[0:128, :], xh[128:256, :]]
                    else:
                        g = gi - 1
                        srcs = [xq[(g * 4 + t) * 128:(g * 4 + t + 1) * 128, :] for t in range(4)]
                    xids[gi] = x_cast(srcs)

                def s1_x(gi):
                    if gi > 8:
                        return
                    x_tr(xids[gi], gi % 2)

                def s1_group(gi):
                    gs = gi % 2
                    if gi == 0:
                        lks = [0, 33]
                    else:
                        g = gi - 1
                        lks = [1 + g * 4 + t for t in range(4)]
                    nt = len(lks)
                    Tn = nt * 128
                    xr = XT[gs][0:nt]
                    s1_xc(gi + 1)
                    pj = [2, 3, 4]
                    pjc = {"i": 0}

                    def proj(c0, M=128):
                        b = pj[pjc["i"] % 3]
                        pjc["i"] += 1
                        for dc in range(8):
                            mm(ps[0:M, b, 0:Tn], wSb[:, dc, c0:c0 + M], xT[gs][:, dc, 0:Tn],
                               dc == 0, dc == 7, xr + [WSB], [PB[b]])
                        return b

                    for kv in range(2):
                        b = proj(512 + kv * 128)
                        if gi == 0:
                            evac(KT[:, kv, 0:128], ps[:, b, 0:128], [PB[b]], [KTB[0]])
                            evac(KT[:, kv, 33 * 128:34 * 128], ps[:, b, 128:256], [PB[b]], [KTB[0]])
                        else:
                            evac(KT[:, kv, lks[0] * 128:lks[0] * 128 + 512], ps[:, b, 0:512], [PB[b]], [KTB[gi]])
                    for t in range(nt):
                        b = 5 + (t % 2)
                        for dc in range(8):
                            mm(ps[:, b, 0:128], xT[gs][:, dc, t * 128:(t + 1) * 128], wSb[:, dc, 768:896],
                               dc == 0, dc == 7, [XT[gs][t], WSB], [PB[b]])
                        vcopy("vector", VA[:, lks[t], :, 0:64],
                              ps[:, b, 0:128].rearrange("p (k d) -> p k d", k=2), [PB[b]], [VAB[lks[t]]])
                    if gi == 0:
                        for j, lk in enumerate((0, 33)):
                            ts("vector", VA[:, lk, :, :], VA[:, lk, :, :], hv_t[:, j:j + 1], None, ALU.mult, None,
                               [CONST, VAB[lk]], [VAB[lk]])
                        s1_x(gi + 1)
                        return
                    g = gi - 1
                    for c in range(4):
                        b = proj(c * 128)
                        evac(QT[:, c, g * 512:(g + 1) * 512], ps[:, b, 0:512], [PB[b]], [QTB[c][g]])
                    s1_x(gi + 1)
                    for c in range(4):
                        b = proj(896 + c * 128)
                        act(mixedT[:, 4 + c, g * 512:(g + 1) * 512], ps[:, b, 0:512], AF.Silu,
                            [PB[b]], [MX[4 + c][g]])

                s1_xc(0)
                s1_x(0)
                for gi in range(9):
                    s1_group(gi)
            P.enabled = True
            P.barrier()

            with ExitStack() as s2:
                P.enabled = "S2" in phases
                bias_t = T(s2, "bias_t", [128, 2, 3, 512], F32)
                esb = T(s2, "esb", [128, 2, 512], F32)
                tmpS = [T(s2, f"tmpS{i}", [128, 3, 512], F32) for i in range(2)]
                TS_ = [Buf(f"tmps{i}") for i in range(2)]
                PTs = [T(s2, f"PTs{i}", [128, 3, 512], BF16) for i in range(2)]
                PTB = [Buf(f"pts{i}") for i in range(2)]
                den = [T(s2, f"den{i}", [128, 512], F32) for i in range(2)]
                DEN = [Buf(f"den{i}") for i in range(2)]
                tmpo = [T(s2, f"tmpo{i}", [128, 512], F32) for i in range(2)]
                TMO = [Buf(f"tmo{i}") for i in range(2)]
                C2 = Buf("c2")

                def gpos(g):
                    return (g % 2) * 2 + g // 2
                ind16 = T(s2, "ind16", [1, 128], BF16)
                esb16 = T(s2, "esb16", [1, 2, 512], BF16)
                memset("vector", ind16[0:1, 0:64], 0.0, [C2])
                memset("vector", ind16[0:1, 64:128], 1.0, [C2])
                for kv in range(2):
                    dma("sync", bias_t[:, kv, :, :].rearrange("p b c -> p (b c)"), biasT[kv][:, :], (), [C2])
                for kv in range(2):
                    for g in range(4):
                        h = kv * 4 + g
                        act(esb[:, kv, gpos(g) * 128:(gpos(g) + 1) * 128], zeros32[:], AF.Exp, [CONST], [C2],
                            bias=sink_t[:, h:h + 1])
                vcopy("vector", esb16[:], esb[0:1, :, :], [C2], [C2])

                def s2_qk(it):
                    n, kv = it // 2, it % 2
                    sl = it % 2
                    kbufs = sorted(set(0 if lk in (0, 33) else 1 + (lk - 1) // 4 for lk in (n, n + 1, n + 2)))
                    lastmm = None
                    for hf in range(2):
                        for kb in range(3):
                            b = sl * 3 + kb
                            m = mm(ps[:, b, hf * 256:(hf + 1) * 256],
                                   KT[hf * 64:(hf + 1) * 64, kv, (n + kb) * 128:(n + kb + 1) * 128],
                                   QT[hf * 64:(hf + 1) * 64, kv * 2:kv * 2 + 2, n * 128:(n + 1) * 128],
                                   True, True,
                                   [KTB[k] for k in kbufs] + [QTB[kv * 2][n // 4], QTB[kv * 2 + 1][n // 4]], [PB[b]],
                                   force=[lastmm] if (hf == 1 and kb == 0) else [])
                            if hf == 0:
                                lastmm = m

                def s2_bias(it):
                    n, kv = it // 2, it % 2
                    sl = it % 2
                    stt(tmpS[sl][:].rearrange("p a b -> p (a b)"),
                        ps[:, sl * 3:sl * 3 + 3, :].rearrange("p a b -> p (a b)"), SWA_SCALE,
                        bias_t[:, kv, :, :].rearrange("p a b -> p (a b)"),
                        ALU.mult, ALU.add, [PB[sl * 3 + kb] for kb in range(3)] + [C2], [TS_[sl]])

                def s2_exp(it):
                    sl = it % 2
                    act(PTs[sl][:], tmpS[sl][:], AF.Exp, [TS_[sl]], [PTB[sl]])

                def s2_pv(it):
                    n, kv = it // 2, it % 2
                    sl = it % 2
                    ob = 6 + sl
                    for kb in range(3):
                        mm(ps[:, ob, :], VA[:, n + kb, kv, :], PTs[sl][:, kb, :], kb == 0, False,
                           [VAB[n + kb], PTB[sl]], [PB[ob]])
                    mm(ps[:, ob, :], ind16[0:1, :], esb16[0:1, kv, :], False, True, [C2], [PB[ob]])

                def s2_rden(it):
                    sl = it % 2
                    ob = 6 + sl
                    act(den[sl][64:128, :], ps[64:128, ob, :], AF.Ln, [PB[ob]], [DEN[sl]])
                    act(den[sl][64:128, :], den[sl][64:128, :], AF.Exp, [DEN[sl]], [DEN[sl]], scale=-1.0)

                def s2_rest(it):
                    n, kv = it // 2, it % 2
                    sl = it % 2
                    ob = 6 + sl
                    for hf in range(2):
                        rows = slice(hf * 64, (hf + 1) * 64)
                        gc = slice(hf * 256, (hf + 1) * 256)
                        tt("vector", tmpo[sl][rows, gc], ps[0:64, ob, gc], den[sl][64:128, gc], ALU.mult,
                           [PB[ob], DEN[sl]], [TMO[sl]])
                    for g in range(4):
                        h = kv * 4 + g
                        c, hf = 4 + h // 2, h % 2
                        rows = slice(hf * 64, (hf + 1) * 64)
                        dst = mixedT[rows, c, n * 128:(n + 1) * 128]
                        tt("gpsimd", dst, tmpo[sl][rows, gpos(g) * 128:(gpos(g) + 1) * 128], dst, ALU.mult,
                           [TMO[sl], MX[c][n // 4]], [MX[c][n // 4]])

                s2_qk(0)
                s2_bias(0)
                s2_exp(0)
                for it in range(64):
                    if it + 1 < 64:
                        s2_qk(it + 1)
                    if it >= 1:
                        s2_rden(it - 1)
                    if it + 1 < 64:
                        s2_bias(it + 1)
                    if it >= 1:
                        s2_rest(it - 1)
                    if it + 1 < 64:
                        s2_exp(it + 1)
                    s2_pv(it)
                s2_rden(63)
                s2_rest(63)
            P.enabled = True
            P.barrier()

        with ExitStack() as sM:
            cqn = T(sM, "cqn", [128, 2, NQ], BF16)
            CQN = [Buf(f"cqn{g}") for g in range(8)]
            ckvn = T(sM, "ckvn", [128, NK], BF16)
            CKV = [Buf(f"ckv{g}") for g in range(16)]
            KTm = T(sM, "KTm", [128, NK], BF16)
            KNB = [Buf(f"knb{g}") for g in range(16)]
            KRB = [Buf(f"krb{g}") for g in range(16)]
            wuqb = T(sM, "wuqb", [128, 2, 1024], BF16)
            wukvb = T(sM, "wukvb", [128, 1024], BF16)
            WQB = Buf("wqb")
            WKB = Buf("wkb")
            cst = [T(sM, f"cst{i}", [128, 512], F32) for i in range(2)]
            CST = [Buf(f"cst{i}") for i in range(2)]
            t12 = [T(sM, f"t12_{i}", [128, 512], F32) for i in range(2)]
            T12 = [Buf(f"t12_{i}") for i in range(2)]
            csn = {"n": 0}

            def rope(b, s, dst, DST):
                tt("vector", t12[s][96:128, :], ps[96:128, b, :], cst[s][96:128, :], ALU.mult,
                   [PB[b], CST[s]], [T12[s]])
                tt("vector", ps[64:96, b, :], ps[64:96, b, :], cst[s][64:96, :], ALU.mult,
                   [PB[b], CST[s]], [PB[b]])
                tt("vector", dst, ps[64:96, b, :], t12[s][96:128, :], ALU.add, [PB[b], T12[s]], [DST])

            with ExitStack() as sa:
                P.enabled = "A" in phases
                alloc_staging(sa, "a")
                xT, XT = stg["xT"], stg["XT"]
                wAb = T(sa, "wAb", [128, 8, WA_COLS], BF16)
                WAB = Buf("wab")
                sq = T(sa, "sq", [128, 3, 512], F32)
                SQ = [Buf(f"sq{i}") for i in range(3)]
                rb = T(sa, "rb", [128, 2, 512], F32)
                RB = Buf("rb")
                load_weight_chunks(wA, WA_COLS, lambda c: wAb[:, c, :], 8, WAB)
                load_weight_chunks(wuq, 1024, lambda c: wuqb[:, c, :], 2, WQB)
                load_weight_chunks(wukv, 1024, lambda c: wukvb[:, :], 1, WKB)

                xids = {}

                def a_xc(kg):
                    if kg > 15:
                        return
                    src = xq if kg < 8 else xo
                    g = kg % 8
                    srcs = [src[(g * 4 + t) * 128:(g * 4 + t + 1) * 128, :] for t in range(4)]
                    xids[kg] = x_cast(srcs)

                def a_x(kg):
                    if kg > 15:
                        return
                    x_tr(xids[kg], kg % 2)

                def a_group(kg):
                    own = kg < 8
                    gs = kg % 2
                    g = kg % 8
                    xr = XT[gs][0:4]
                    a_xc(kg + 1)

                    def proj(b, c0, M=128):
                        for dc in range(8):
                            mm(ps[0:M, b, :], wAb[:, dc, c0:c0 + M], xT[gs][:, dc, :],
                               dc == 0, dc == 7, xr + [WAB], [PB[b]])

                    proj(4, 256)
                    act(sq[:, 2, :], ps[:, 4, :], AF.Square, [PB[4]], [SQ[2]])
                    mm(ps[:, 6, :], ones32[:], sq[:, 2, :], True, True, [CONST, SQ[2]], [PB[6]])
                    act(rb[:, 1, :], ps[:, 6, :], AF.Ln, [PB[6]], [RB], scale=1.0 / 128.0, bias=eps_t[:, 0:1])
                    act(rb[:, 1, :], rb[:, 1, :], AF.Exp, [RB], [RB], scale=-0.5)
                    if own:
                        proj(2, 0)
                        proj(3, 128)
                        act(sq[:, 0, :], ps[:, 2, :], AF.Square, [PB[2]], [SQ[0]])
                        act(sq[:, 1, :], ps[:, 3, :], AF.Square, [PB[3]], [SQ[1]])
                        mm(ps[:, 5, :], ones32[:], sq[:, 0, :], True, False, [CONST, SQ[0]], [PB[5]])
                        mm(ps[:, 5, :], ones32[:], sq[:, 1, :], False, True, [CONST, SQ[1]], [PB[5]])
                        act(rb[:, 0, :], ps[:, 5, :], AF.Ln, [PB[5]], [RB], scale=1.0 / 256.0, bias=eps_t[:, 0:1])
                        act(rb[:, 0, :], rb[:, 0, :], AF.Exp, [RB], [RB], scale=-0.5)
                        for cc in range(2):
                            stt(cqn[:, cc, g * 512:(g + 1) * 512], ps[:, 2 + cc, :], gq_t[:, cc:cc + 1], rb[:, 0, :],
                                ALU.mult, ALU.mult, [PB[2 + cc], RB, CONST], [CQN[g]])
                    stt(ckvn[:, kg * 512:(kg + 1) * 512], ps[:, 4, :], gkv_t[:, 0:1], rb[:, 1, :],
                        ALU.mult, ALU.mult, [PB[4], RB, CONST], [CKV[kg]])
                    proj(7, 384)
                    s = csn["n"] % 2
                    csn["n"] += 1
                    dma("sync", cst[s][64:128, :], cs[kg, :, :], (), [CST[s]])
                    rope(7, s, KTm[64:96, kg * 512:(kg + 1) * 512], KRB[kg])
                    a_x(kg + 1)
                    if own:
                        for c in range(4):
                            b = (5, 6, 7)[c % 3]
                            proj(b, 512 + c * 128)
                            act(mixedT[:, c, g * 512:(g + 1) * 512], ps[:, b, :], AF.Silu, [PB[b]], [MX[c][g]])

                a_xc(0)
                a_x(0)
                for kg in range(16):
                    a_group(kg)
            P.enabled = True
            P.barrier()

            with ExitStack() as sc:
                P.enabled = "C" in phases
                VAm = T(sc, "VAm", [128, 64, 128], BF16)
                VMB = [Buf(f"vmb{i}") for i in range(8)]
                QTm = T(sc, "QTm", [128, NQ], BF16)
                QMB = [Buf(f"qmb{i}") for i in range(8)]
                PT = [T(sc, f"PT{i}", [128, 3, 512], BF16) for i in range(3)]
                PTB = [Buf(f"ptb{i}") for i in range(3)]
                rden = [T(sc, f"rden{i}", [128, 512], F32) for i in range(2)]
                RDN = [Buf(f"rdn{i}") for i in range(2)]
                tmpo = [T(sc, f"tmpoC{i}", [128, 512], F32) for i in range(2)]
                TMO = [Buf(f"tmoC{i}") for i in range(2)]

                def vo(h):
                    return (0, 64) if h % 2 == 0 else (64, 0)

                def pcopy(out, in_, reads, writes, up):
                    if up:
                        evac(out, in_, reads, writes)
                    else:
                        vcopy("vector", out, in_, reads, writes)

                def prep_k(h, kg, b, up=False):
                    mm(ps[0:64, b, :], wukvb[:, h * 128:h * 128 + 64], ckvn[:, kg * 512:(kg + 1) * 512],
                       True, True, [WKB, CKV[kg]], [PB[b]])
                    pcopy(KTm[0:64, kg * 512:(kg + 1) * 512], ps[0:64, b, :], [PB[b]], [KNB[kg]], up)

                def ones_v(h, vb, b=None):
                    voff, ooff = vo(h)
                    memset("gpsimd", VAm[:, vb * 8:(vb + 1) * 8, ooff:ooff + 64], 1.0, [VMB[vb]])

                def prep_v(h, vb, b, up=False):
                    voff, ooff = vo(h)
                    for j in range(8):
                        kblk = vb * 8 + j
                        mm(ps[:, b, j * 64:(j + 1) * 64], ckvn[:, kblk * 128:(kblk + 1) * 128],
                           wukvb[:, h * 128 + 64:h * 128 + 128], True, True,
                           [WKB, CKV[kblk // 4]], [PB[b]])
                    if up:
                        ones_v(h, vb)
                    pcopy(VAm[:, vb * 8:(vb + 1) * 8, voff:voff + 64],
                          ps[:, b, :].rearrange("p (j d) -> p j d", j=8), [PB[b]], [VMB[vb]], up)

                def cs_dma(h, qg, b=None):
                    s = (h * 8 + qg) % 2
                    dma("sync", cst[s][64:128, :], cs[qg, :, :], (), [CST[s]])

                def prep_q(h, qg, b, up=False):
                    for cc in range(2):
                        mm(ps[:, b, :], wuqb[:, cc, h * 128:(h + 1) * 128], cqn[:, cc, qg * 512:(qg + 1) * 512],
                           cc == 0, cc == 1, [WQB, CQN[qg]], [PB[b]])
                    pcopy(QTm[0:64, qg * 512:(qg + 1) * 512], ps[0:64, b, :], [PB[b]], [QMB[qg]], up)
                    rope(b, (h * 8 + qg) % 2, QTm[64:96, qg * 512:(qg + 1) * 512], QMB[qg])

                for qg in range(8):
                    cs_dma(0, qg)
                    prep_q(0, qg, qg % 6, True)
                for kg in range(16):
                    prep_k(0, kg, kg % 6, True)
                for vb in range(8):
                    prep_v(0, vb, vb % 6, True)

                hooks = {}
                now_hooks = {}
                for h in range(7):
                    for qg in range(8):
                        now_hooks.setdefault((h, qg, 20), []).append((cs_dma, h + 1, qg))
                        hooks.setdefault((h, qg, 63), []).append((prep_q, h + 1, qg))
                    for kg in range(16):
                        hooks.setdefault((h, 7, 4 * kg + 3), []).append((prep_k, h + 1, kg))
                    for vb in range(8):
                        now_hooks.setdefault((h, 7, 8 * vb + 7), []).append((ones_v, h + 1, vb))
                        hooks.setdefault((h, 7, 8 * vb + 7), []).append((prep_v, h + 1, vb))
                pending = []
                tcnt = {"n": 0}

                def fin_part(h, qg, c):
                    voff, ooff = vo(h)
                    s, ob = qg % 2, 6 + qg % 2
                    orow = slice(voff, voff + 64)
                    srow = slice(ooff, ooff + 64)
                    cc_ = slice(c * 128, (c + 1) * 128)
                    recip(rden[s][srow, cc_], ps[srow, ob, cc_], [PB[ob]], [RDN[s]])
                    tt("vector", tmpo[s][orow, cc_], ps[orow, ob, cc_], rden[s][srow, cc_], ALU.mult,
                       [PB[ob], RDN[s]], [TMO[s]])
                    dst = mixedT[orow, h // 2, qg * 512 + c * 128:qg * 512 + (c + 1) * 128]
                    tt("gpsimd", dst, tmpo[s][orow, cc_], dst, ALU.mult,
                       [TMO[s], MX[h // 2][qg]], [MX[h // 2][qg]])

                tiles = [(h, qg, kb) for h in range(8) for qg in range(8) for kb in range(64)]
                glist = []
                tpos = {"i": 0}
                last_prep = {"j": -100}

                def prep_would_fire(groups):
                    pq = [it_[0] for it_ in pending]
                    n_ = tcnt["n"]
                    for g_ in groups:
                        for (h, qg, kb) in g_:
                            if kb == 63:
                                pq += [fin_part] * 4
                            for fn, hh, idx in hooks.get((h, qg, kb), ()):
                                pq.append(fn)
                            n_ += 1
                            if pq and n_ % 3 == 0:
                                if pq.pop(0) is not fin_part:
                                    return True
                    return False

                def form_group(j):
                    if tpos["i"] >= len(tiles):
                        return False
                    assert j == len(glist)
                    n_ = 3
                    if j % 2 == 1:
                        prev = [glist[x] for x in (j - 2, j - 1) if x >= 0]
                        if last_prep["j"] > j - 6 or prep_would_fire(prev):
                            n_ = 2
                    glist.append(tiles[tpos["i"]:tpos["i"] + n_])
                    tpos["i"] += n_
                    return True

                def emit_qk(j):
                    sb0 = 0 if j % 2 == 0 else 3
                    for i, (h, qg, kb) in enumerate(glist[j]):
                        mm(ps[:, sb0 + i, :], KTm[0:96, kb * 128:(kb + 1) * 128],
                           QTm[0:96, qg * 512:(qg + 1) * 512], True, True,
                           [KNB[kb // 4], KRB[kb // 4], QMB[qg]], [PB[sb0 + i]])

                def emit_exp(j):
                    sb0 = 0 if j % 2 == 0 else 3
                    pslot = j % 3
                    ng = len(glist[j])
                    act(PT[pslot][:, 0:ng, :], ps[:, sb0:sb0 + ng, :], AF.Exp,
                        [PB[sb0 + i] for i in range(ng)], [PTB[pslot]], scale=MLA_SCALE)

                def emit_pv(j):
                    pslot = j % 3
                    gt = glist[j]
                    for i, (h, qg, kb) in enumerate(gt):
                        voff, ooff = vo(h)
                        ob = 6 + qg % 2
                        mm(ps[:, ob, :], VAm[:, kb, :], PT[pslot][:, i, :], kb == 0, kb == 63,
                           [VMB[kb // 8], PTB[pslot]], [PB[ob]])
                        for fn, hh, idx in now_hooks.get((h, qg, kb), ()):
                            fn(hh, idx)
                        if kb == 63:
                            for c in range(4):
                                pending.append((fin_part, h, qg, c))
                        for fn, hh, idx in hooks.get((h, qg, kb), ()):
                            pending.append((fn, hh, idx, 5))
                        tcnt["n"] += 1
                        if pending and tcnt["n"] % 3 == 0:
                            it_ = pending.pop(0)
                            if it_[0] is not fin_part:
                                last_prep["j"] = j
                            it_[0](*it_[1:])

                form_group(0)
                emit_qk(0)
                form_group(1)
                emit_qk(1)
                j = 0
                while j < len(glist):
                    emit_exp(j)
                    if form_group(j + 2):
                        emit_qk(j + 2)
                    emit_pv(j)
                    j += 1
                while pending:
                    it_ = pending.pop(0)
                    it_[0](*it_[1:])
            P.enabled = True
            P.barrier()

        with ExitStack() as sd:
            alloc_staging(sd, "d", full=False)
            xstage, XS = stg["xstage"], stg["XS"]
            woutb = T(sd, "woutb", [128, 8, 1024], BF16)
            WOB = Buf("wob")
            lng_t = T(sd, "lng_t", [128, 1024], F32)
            lnb_t = T(sd, "lnb_t", [128, 1024], F32)
            LNC = Buf("lnc")
            NS = 4
            junk = T(sd, "junk", [128, 1024], BF16)
            JNK = Buf("junk")
            yb = [T(sd, f"yb{i}", [128, 1024], F32) for i in range(NS)]
            YB = [Buf(f"yb{i}") for i in range(NS)]
            st6 = [T(sd, f"st6_{i}", [128, 12], F32) for i in range(NS)]
            ST6 = [Buf(f"st6_{i}") for i in range(NS)]
            mv = [T(sd, f"mv{i}", [128, 2], F32) for i in range(NS)]
            MV = [Buf(f"mv{i}") for i in range(NS)]
            rstd = [T(sd, f"rstd{i}", [128, 1], F32) for i in range(NS)]
            RSD = [Buf(f"rsd{i}") for i in range(NS)]
            nmr = [T(sd, f"nmr{i}", [128, 1], F32) for i in range(NS)]
            NMR = [Buf(f"nmr{i}") for i in range(NS)]
            load_weight_chunks(wout, 1024, lambda c: woutb[:, c, :], 8, WOB)
            dma("sync", lng_t[:], lng.partition_broadcast(128), (), [LNC])
            dma("sync", lnb_t[:], lnb.partition_broadcast(128), (), [LNC])
            stores = []
            NT = NQ // 128

            def load_x(t):
                if t < NT:
                    s3 = t % 3
                    dma("sync", xstage[s3][:], xq[t * 128:(t + 1) * 128, :], (), [XS[s3]])

            def d_a(t):
                s, s3 = t % NS, t % 3
                for hh in range(2):
                    b = (t % 2) * 2 + hh
                    for k in range(8):
                        mm(ps[:, b, :], mixedT[:, k, t * 128:(t + 1) * 128], woutb[:, k, hh * 512:(hh + 1) * 512],
                           k == 0, k == 7, [MX[k][t // 4], WOB], [PB[b]])
                    stt(yb[s][:, hh * 512:(hh + 1) * 512], xstage[s3][:, hh * 512:(hh + 1) * 512], ALPHA,
                        ps[:, b, :], ALU.mult, ALU.add, [XS[s3], PB[b]], [YB[s]])
                act(junk[:], yb[s][:], AF.Identity, [YB[s]], [JNK, ST6[s]], accum_out=st6[s][:, 0:1])
                act(junk[:], yb[s][:], AF.Square, [YB[s]], [JNK, ST6[s]], accum_out=st6[s][:, 1:2])

            def d_a2(t):
                s = t % NS
                ts("vector", mv[s][:, 0:1], st6[s][:, 0:1], 1.0 / D_MODEL, None, ALU.mult, None, [ST6[s]], [MV[s]])
                tt("vector", st6[s][:, 2:3], mv[s][:, 0:1], mv[s][:, 0:1], ALU.mult, [MV[s]], [ST6[s]])
                stt(mv[s][:, 1:2], st6[s][:, 1:2], 1.0 / D_MODEL, st6[s][:, 2:3], ALU.mult, ALU.subtract,
                    [ST6[s]], [MV[s]])
                act(rstd[s][:], mv[s][:, 1:2], AF.Sqrt, [MV[s]], [RSD[s]], bias=1e-5)

            def d_b1(t):
                s = t % NS
                recip(rstd[s][:], rstd[s][:], [RSD[s]], [RSD[s]])
                ts("vector", nmr[s][:], mv[s][:, 0:1], rstd[s][:, 0:1], -1.0, ALU.mult, ALU.mult,
                   [MV[s], RSD[s]], [NMR[s]])
                act(yb[s][:], yb[s][:], AF.Identity, [YB[s], RSD[s], NMR[s]], [YB[s]],
                    scale=rstd[s][:, 0:1], bias=nmr[s][:, 0:1])

            def d_b2(t):
                s = t % NS
                tt("vector", yb[s][:], yb[s][:], lng_t[:], ALU.mult, [YB[s], LNC], [YB[s]])
                tt("gpsimd", yb[s][:], yb[s][:], lnb_t[:], ALU.add, [YB[s], LNC], [YB[s]])
                stores.append(dma("sync", y[t * 128:(t + 1) * 128, :], yb[s][:], [YB[s]], []))

            load_x(0)
            load_x(1)
            load_x(2)
            d_a(0)
            d_a2(0)
            d_a(1)
            d_a2(1)
            d_b1(0)
            for t in range(NT):
                load_x(t + 3)
                if t + 2 < NT:
                    d_a(t + 2)
                d_b2(t)
                if t + 1 < NT:
                    d_b1(t + 1)
                if t + 2 < NT:
                    d_a2(t + 2)
            P.op("sync", None, deps=stores)
            P.op("scalar", None, deps=stores)

            with ExitStack() as se:
                sems = {e: se.enter_context(nc.semaphore(f"s_{e}")) for e in Prog.ENGS}
                dma_sems = {e: [se.enter_context(nc.semaphore(f"d_{e}{i}")) for i in range(Prog.DMA_POOL)]
                            for e in Prog.ENGS}
                P.finalize_tokens(sems, dma_sems)
                block = se.enter_context(nc.Block())

                @block.sync
                def _(e):
                    P.emit("sync", e)

                @block.tensor
                def _(e):
                    P.emit("tensor", e)

                @block.vector
                def _(e):
                    P.emit("vector", e)

                @block.scalar
                def _(e):
                    P.emit("scalar", e)

                @block.gpsimd
                def _(e):
                    P.emit("gpsimd", e)
    return nc


_CACHE = {}


def kernel(x, w_in, g_q, g_kv, w_uq, w_ukv, sink, rel_bias, w_out, ln_g, ln_b):
    x = np.asarray(x, np.float32)
    w_in = np.asarray(w_in, np.float32)
    w_uq = np.asarray(w_uq, np.float32)
    w_ukv = np.asarray(w_ukv, np.float32)
    w_out = np.asarray(w_out, np.float32)
    rel_bias = np.asarray(rel_bias, np.float32)
    o_cq, o_ckv, o_kr, o_ga, o_qs, o_ks, o_vs, o_gb = 0, 256, 384, 416, 928, 1440, 1568, 1696
    kr_cols = np.arange(o_kr, o_kr + 32)
    kr_sw = np.concatenate([kr_cols[16:], kr_cols[:16]])
    colsA = np.concatenate([np.arange(o_cq, o_cq + 256), np.arange(o_ckv, o_ckv + 128), kr_cols, kr_sw,
                            kr_cols, kr_sw, np.arange(o_ga, o_ga + 512)])
    k0 = np.arange(o_ks, o_ks + 64)
    k1 = np.arange(o_ks + 64, o_ks + 128)
    colsS = np.concatenate([np.arange(o_qs, o_qs + 512), k0, k0, k1, k1, np.arange(o_vs, o_vs + 128),
                            np.arange(o_gb, o_gb + 512)])
    wA = np.ascontiguousarray(w_in[:, colsA])
    wS = np.ascontiguousarray(w_in[:, colsS])
    cq = []
    for h in range(8):
        base = h * 96
        rope = np.arange(base + 64, base + 96)
        cq.append(np.concatenate([np.arange(base, base + 64), rope, rope[16:], rope[:16]]))
    wuq = np.ascontiguousarray(w_uq[:, np.concatenate(cq)])
    gq = np.ascontiguousarray(np.asarray(g_q, np.float32).reshape(2, 128).T)
    gkv = np.ascontiguousarray(np.asarray(g_kv, np.float32).reshape(128, 1))
    q_loc = np.arange(128)
    k_loc = np.arange(384) - 128
    rel = k_loc[None, :] - q_loc[:, None]
    band = np.abs(rel) <= 128
    bucket = _t5_bucket(rel)
    bias = rel_bias[bucket]
    bias = np.where(band[:, :, None], bias, np.float32(NEG)).astype(np.float32)
    bT = bias.reshape(128, 3, 128, 2, 4).transpose(2, 3, 1, 4, 0)
    bT = bT[:, :, :, [0, 2, 1, 3], :]
    biasT = np.ascontiguousarray(bT.reshape(128, 3072))
    ident = np.eye(128, dtype=np.float32)

    in_maps = []
    for c in range(N_CORES):
        b, half = c // 2, c % 2
        own0 = half * NQ
        oth0 = (1 - half) * NQ
        xq_c = x[b, own0:own0 + NQ]
        xo_c = x[b, oth0:oth0 + NQ]
        xh_c = np.zeros((256, D_MODEL), np.float32)
        hv_c = np.zeros((128, 2), np.float32)
        if own0 - 128 >= 0:
            xh_c[0:128] = x[b, own0 - 128:own0]
            hv_c[:, 0] = 1.0
        if own0 + NQ + 128 <= SEQ:
            xh_c[128:256] = x[b, own0 + NQ:own0 + NQ + 128]
            hv_c[:, 1] = 1.0
        pos = np.concatenate([np.arange(own0, own0 + NQ), np.arange(oth0, oth0 + NQ)])
        in_maps.append({
            "xq": np.ascontiguousarray(xq_c), "xo": np.ascontiguousarray(xo_c), "xh": xh_c, "hv": hv_c,
            "wukv": w_ukv, "gq": gq, "gkv": gkv,
            "cs": _rope_tables(pos), "sink": np.asarray(sink, np.float32),
            **{f"wA{i}": wA[i * 128:(i + 1) * 128] for i in range(8)},
            **{f"wS{i}": wS[i * 128:(i + 1) * 128] for i in range(8)},
            **{f"wuq{i}": wuq[i * 128:(i + 1) * 128] for i in range(2)},
            **{f"wout{i}": w_out[i * 128:(i + 1) * 128] for i in range(8)},
            **{f"biasT{kv}": np.ascontiguousarray(biasT[:, kv * 1536:(kv + 1) * 1536]) for kv in range(2)},
            "lng": np.asarray(ln_g, np.float32), "lnb": np.asarray(ln_b, np.float32), "ident": ident,
        })
    if "nc" not in _CACHE:
        _CACHE["nc"] = build_program()
    res = run_bass_kernel_spmd(_CACHE["nc"], in_maps, core_ids=list(range(N_CORES)))
    out = np.empty((BATCH, SEQ, D_MODEL), np.float32)
    for c in range(N_CORES):
        b, half = c // 2, c % 2
        out[b, half * NQ:(half + 1) * NQ] = res.results[c]["y"]
    return out
```

```python
import math
from contextlib import ExitStack

import numpy as np
import concourse.bass as bass
import concourse.mybir as mybir
from concourse.bass_utils import run_bass_kernel_spmd

F32 = mybir.dt.float32
BF16 = mybir.dt.bfloat16
AF = mybir.ActivationFunctionType
ALU = mybir.AluOpType

D_MODEL = 1024
BATCH = 4
SEQ = 8192
NQ = 4096
NK = 8192
MLA_SCALE = 1.0 / math.sqrt(96.0)
SWA_SCALE = 0.125
ALPHA = 2.0 ** 0.25
N_CORES = 8
NEG = -30000.0

WA_COLS = 1024
WS_COLS = 1408


class Buf:
    __slots__ = ("name", "w", "r")

    def __init__(self, name):
        self.name = name
        self.w = None
        self.r = []


class Op:
    __slots__ = ("eng", "fn", "deps", "users", "is_dma", "sem", "val", "qidx")


class Prog:
    ENGS = ("tensor", "vector", "scalar", "gpsimd", "sync")
    DMA_POOL = 8

    def __init__(self):
        self.ops = {e: [] for e in self.ENGS}
        self.ndma = {e: 0 for e in self.ENGS}
        self.dma_last = {}
        self.last = {e: None for e in self.ENGS}
        self.enabled = True

    def op(self, eng, fn, reads=(), writes=(), deps=(), dma=False, force=()):
        if not self.enabled:
            return None
        o = Op()
        o.eng, o.fn, o.is_dma, o.users = eng, fn, dma, False
        o.sem = o.val = o.qidx = None
        d = set(x for x in deps if x is not None)
        for b in tuple(reads) + tuple(writes):
            if b.w is not None:
                d.add(b.w)
        for b in writes:
            d.update(b.r)
        d.discard(o)
        if dma:
            o.qidx = self.ndma[eng]
            self.ndma[eng] += 1
            prev = self.dma_last.get((eng, o.qidx % self.DMA_POOL))
            if prev is not None:
                d.add(prev)
            self.dma_last[(eng, o.qidx % self.DMA_POOL)] = o
            o.users = True
        o.deps = [x for x in d if not (x.eng == "tensor" and eng == "tensor" and not x.is_dma)]
        o.deps += [x for x in force if x is not None and x not in o.deps]
        for x in o.deps:
            x.users = True
        for b in reads:
            b.r.append(o)
        for b in writes:
            b.w = o
            b.r = []
        self.ops[eng].append(o)
        if fn is not None:
            self.last[eng] = o
        return o

    def barrier(self):
        lasts = [self.last[e] for e in self.ENGS if self.last[e] is not None]
        lasts += list(self.dma_last.values())
        for e in self.ENGS:
            self.op(e, None, deps=lasts)

    def finalize_tokens(self, sems, dma_sems):
        for e in self.ENGS:
            cnt = 0
            for o in self.ops[e]:
                if o.is_dma:
                    o.sem = dma_sems[e][o.qidx % self.DMA_POOL]
                    o.val = 16 * (o.qidx // self.DMA_POOL + 1)
                elif o.users:
                    cnt += 1
                    o.sem = sems[e]
                    o.val = cnt

    def emit(self, eng_name, e):
        waited = {}
        for o in self.ops[eng_name]:
            need = {}
            for x in o.deps:
                k = id(x.sem)
                if k not in need or need[k][1] < x.val:
                    need[k] = (x.sem, x.val)
            for k, (sem, val) in need.items():
                if waited.get(k, 0) >= val:
                    continue
                e.wait_ge(sem, val)
                waited[k] = val
            if o.fn is None:
                continue
            inst = o.fn(e)
            if o.is_dma:
                inst.then_inc(o.sem, 16)
            elif o.users:
                inst.then_inc(o.sem, 1)


def _t5_bucket(rel):
    half = 16
    ret = np.where(rel > 0, half, 0)
    n = np.abs(rel)
    max_exact = half // 2
    large = max_exact + (np.log(np.maximum(n, 1).astype(np.float32) / max_exact)
                         / np.log(128 / max_exact) * (half - max_exact)).astype(np.int32)
    large = np.minimum(large, half - 1)
    return (ret + np.where(n < max_exact, n, large)).astype(np.int32)


def _rope_tables(positions):
    inv_freq = 10000.0 ** (-np.arange(0, 32, 2, dtype=np.float64) / 32.0)
    ang = positions.astype(np.float64)[:, None] * inv_freq[None, :]
    cos = np.cos(ang).astype(np.float32)
    sin = np.sin(ang).astype(np.float32)
    cos2 = np.concatenate([cos, cos], axis=1)
    sin2 = np.concatenate([-sin, sin], axis=1)
    tab = np.concatenate([cos2, sin2], axis=1)
    ng = positions.shape[0] // 512
    return np.ascontiguousarray(tab.reshape(ng, 512, 64).transpose(0, 2, 1))


def build_program(phases=("S1", "S2", "A", "C", "D")):
    nc = bass.Bass("TRN2", target_bir_lowering=False)
    dt = nc.dram_tensor
    xq = dt("xq", [NQ, D_MODEL], F32, kind="ExternalInput").ap()
    xo = dt("xo", [NQ, D_MODEL], F32, kind="ExternalInput").ap()
    xh = dt("xh", [256, D_MODEL], F32, kind="ExternalInput").ap()
    hv = dt("hv", [128, 2], F32, kind="ExternalInput").ap()
    wA = [dt(f"wA{c}", [128, WA_COLS], F32, kind="ExternalInput").ap() for c in range(8)]
    wS = [dt(f"wS{c}", [128, WS_COLS], F32, kind="ExternalInput").ap() for c in range(8)]
    wuq = [dt(f"wuq{c}", [128, 1024], F32, kind="ExternalInput").ap() for c in range(2)]
    wukv = [dt("wukv", [128, 1024], F32, kind="ExternalInput").ap()]
    wout = [dt(f"wout{c}", [128, 1024], F32, kind="ExternalInput").ap() for c in range(8)]
    gq = dt("gq", [128, 2], F32, kind="ExternalInput").ap()
    gkv = dt("gkv", [128, 1], F32, kind="ExternalInput").ap()
    cs = dt("cs", [16, 64, 512], F32, kind="ExternalInput").ap()
    biasT = [dt(f"biasT{kv}", [128, 1536], F32, kind="ExternalInput").ap() for kv in range(2)]
    sink = dt("sink", [8], F32, kind="ExternalInput").ap()
    lng = dt("lng", [1024], F32, kind="ExternalInput").ap()
    lnb = dt("lnb", [1024], F32, kind="ExternalInput").ap()
    ident = dt("ident", [128, 128], F32, kind="ExternalInput").ap()
    y = dt("y", [NQ, D_MODEL], F32, kind="ExternalOutput").ap()

    P = Prog()
    cnt = {"ev": 0}

    def mm(out, lhsT, rhs, start, stop, reads, writes, force=()):
        return P.op("tensor", lambda e: e.matmul(out, lhsT=lhsT, rhs=rhs, start=start, stop=stop),
                    reads, writes, force=force)

    def tr(out, in_, idn, reads, writes):
        return P.op("tensor", lambda e: e.transpose(out=out, in_=in_, identity=idn), reads, writes)

    def act(out, in_, func, reads, writes, scale=None, bias=None, accum_out=None):
        kw = {}
        if scale is not None:
            kw["scale"] = scale
        if bias is not None:
            kw["bias"] = bias
        if accum_out is not None:
            kw["accum_out"] = accum_out
        return P.op("scalar", lambda e: e.activation(out=out, in_=in_, func=func, **kw), reads, writes)

    def vcopy(eng, out, in_, reads, writes):
        if eng == "scalar":
            return P.op("scalar", lambda e: e.copy(out=out, in_=in_), reads, writes)
        return P.op(eng, lambda e: e.tensor_copy(out=out, in_=in_), reads, writes)

    def evac(out, in_, reads, writes):
        cnt["ev"] += 1
        return vcopy("vector" if cnt["ev"] % 2 else "scalar", out, in_, reads, writes)

    def tt(eng, out, in0, in1, op, reads, writes):
        return P.op(eng, lambda e: e.tensor_tensor(out=out, in0=in0, in1=in1, op=op), reads, writes)

    def stt(out, in0, scalar, in1, op0, op1, reads, writes):
        return P.op("vector", lambda e: e.scalar_tensor_tensor(out=out, in0=in0, scalar=scalar, in1=in1,
                                                                op0=op0, op1=op1), reads, writes)

    def ts(eng, out, in0, s1, s2, op0, op1, reads, writes):
        if op1 is None:
            return P.op(eng, lambda e: e.tensor_scalar(out=out, in0=in0, scalar1=s1, scalar2=None, op0=op0),
                        reads, writes)
        return P.op(eng, lambda e: e.tensor_scalar(out=out, in0=in0, scalar1=s1, scalar2=s2, op0=op0, op1=op1),
                    reads, writes)

    def recip(out, in_, reads, writes):
        return P.op("vector", lambda e: e.reciprocal(out=out, in_=in_), reads, writes)

    def memset(eng, ap, val, writes):
        return P.op(eng, lambda e: e.memset(ap, val), (), writes)

    def dma(eng, out, in_, reads, writes):
        return P.op(eng, lambda e: e.dma_start(out=out, in_=in_), reads, writes, dma=True)

    with ExitStack() as top:
        def T(es, name, shape, dtype):
            return es.enter_context(nc.sbuf_tensor(name, shape, dtype))

        ps = top.enter_context(nc.psum_tensor("ps", [128, 8, 512], F32))
        psb = ps.bitcast(BF16)
        PB = [Buf(f"pb{i}") for i in range(8)]

        mixedT = T(top, "mixedT", [128, 8, NQ], BF16)
        MX = [[Buf(f"mx{c}_{g}") for g in range(8)] for c in range(8)]
        identf = T(top, "identf", [128, 128], F32)
        identb = T(top, "identb", [128, 128], BF16)
        ones32 = T(top, "ones32", [128, 128], F32)
        zeros32 = T(top, "zeros32", [128, 128], F32)
        gq_t = T(top, "gq_t", [128, 2], F32)
        gkv_t = T(top, "gkv_t", [128, 1], F32)
        hv_t = T(top, "hv_t", [128, 2], F32)
        sink_t = T(top, "sink_t", [128, 8], F32)
        CONST = Buf("const")
        stg = {}

        def alloc_staging(es, tag, full=True):
            stg["xstage"] = [T(es, f"xstage{tag}{i}", [128, 1024], F32) for i in range(3)]
            stg["XS"] = [Buf(f"xs{i}") for i in range(3)]
            stg["wstage"] = [T(es, f"wstage{tag}{i}", [128, WS_COLS], F32) for i in range(2)]
            stg["WSTG"] = [Buf(f"wstg{i}") for i in range(2)]
            if full:
                stg["xbt"] = [T(es, f"xbt{tag}{i}", [128, 1024], BF16) for i in range(4)]
                stg["XB"] = [Buf(f"xb{i}") for i in range(4)]
                stg["xT"] = [T(es, f"xT{tag}{i}", [128, 8, 512], BF16) for i in range(2)]
                stg["XT"] = [[Buf(f"xt{i}_{t}") for t in range(4)] for i in range(2)]

        dma("sync", identf[:], ident[:, :], (), [CONST])
        dma("sync", gq_t[:], gq[:, :], (), [CONST])
        dma("sync", gkv_t[:], gkv[:, :], (), [CONST])
        dma("sync", hv_t[:], hv[:, :], (), [CONST])
        dma("sync", sink_t[:], sink.partition_broadcast(128), (), [CONST])
        vcopy("vector", identb[:], identf[:], [CONST], [CONST])
        memset("vector", ones32[:], 1.0, [CONST])
        memset("vector", zeros32[:], 0.0, [CONST])
        eps_t = T(top, "eps_t", [128, 1], F32)
        memset("vector", eps_t[:], 1e-6, [CONST])

        xcount = {"n": 0, "g": 0}

        def load_weight_chunks(src, ncols, dst_fn, nchunks, WB):
            wstage, WSTG = stg["wstage"], stg["WSTG"]
            for c in range(nchunks):
                s = c % 2
                dma("sync", wstage[s][:, 0:ncols], src[c][:, :], (), [WSTG[s]])
                vcopy("scalar" if c % 2 == 0 else "vector", dst_fn(c), wstage[s][:, 0:ncols], [WSTG[s]], [WB])

        def x_cast(srcs):
            xstage, XS, xbt, XB = (stg[k] for k in ("xstage", "XS", "xbt", "XB"))
            ids = []
            for src in srcs:
                i = xcount["n"]
                xcount["n"] += 1
                s3, s4 = i % 3, i % 4
                dma("sync", xstage[s3][:], src, (), [XS[s3]])
                vcopy("vector" if i % 2 else "scalar", xbt[s4][:], xstage[s3][:], [XS[s3]], [XB[s4]])
                ids.append(i)
            return ids

        def x_tr(ids, gs):
            xbt, XB, xT, XT = (stg[k] for k in ("xbt", "XB", "xT", "XT"))
            for t, i in enumerate(ids):
                s2, s4 = i % 2, i % 4
                bank = s2
                for c in range(8):
                    tr(psb[:, bank, c * 128:(c + 1) * 128], xbt[s4][:, c * 128:(c + 1) * 128], identb[:],
                       [XB[s4], CONST], [PB[bank]])
                evac(xT[gs][:, :, t * 128:(t + 1) * 128],
                     psb[:, bank, 0:1024].rearrange("p (c k) -> p c k", c=8),
                     [PB[bank]], [XT[gs][t]])

        with ExitStack() as sS:
            QT = T(sS, "QT", [128, 4, NQ], BF16)
            QTB = [[Buf(f"qt{c}_{g}") for g in range(8)] for c in range(4)]
            KT = T(sS, "KT", [128, 2, 34 * 128], BF16)
            KTB = [Buf(f"kt{g}") for g in range(9)]
            VA = T(sS, "VA", [128, 34, 2, 128], BF16)
            VAB = [Buf(f"va{lk}") for lk in range(34)]
            for lk in range(34):
                memset("gpsimd", VA[:, lk, :, 64:128], 1.0, [VAB[lk]])

            with ExitStack() as s1:
                P.enabled = "S1" in phases
                alloc_staging(s1, "s")
                xT, XT = stg["xT"], stg["XT"]
                wSb = T(s1, "wSb", [128, 8, WS_COLS], BF16)
                WSB = Buf("wsb")
                load_weight_chunks(wS, WS_COLS, lambda c: wSb[:, c, :], 8, WSB)

                xids = {}

                def s1_xc(gi):
                    if gi > 8:
                        return
                    if gi == 0:
                        srcs = [xh[0:128, :], xh[128:256, :]]
                    else:
                        g = gi - 1
                        srcs = [xq[(g * 4 + t) * 128:(g * 4 + t + 1) * 128, :] for t in range(4)]
                    xids[gi] = x_cast(srcs)

                def s1_x(gi):
                    if gi > 8:
                        return
                    x_tr(xids[gi], gi % 2)

                def s1_group(gi):
                    gs = gi % 2
                    if gi == 0:
                        lks = [0, 33]
                    else:
                        g = gi - 1
                        lks = [1 + g * 4 + t for t in range(4)]
                    nt = len(lks)
                    Tn = nt * 128
                    xr = XT[gs][0:nt]
                    s1_xc(gi + 1)
                    pj = [2, 3, 4]
                    pjc = {"i": 0}

                    def proj(c0, M=128):
                        b = pj[pjc["i"] % 3]
                        pjc["i"] += 1
                        for dc in range(8):
                            mm(ps[0:M, b, 0:Tn], wSb[:, dc, c0:c0 + M], xT[gs][:, dc, 0:Tn],
                               dc == 0, dc == 7, xr + [WSB], [PB[b]])
                        return b

                    for kv in range(2):
                        b = proj(512 + kv * 128)
                        if gi == 0:
                            evac(KT[:, kv, 0:128], ps[:, b, 0:128], [PB[b]], [KTB[0]])
                            evac(KT[:, kv, 33 * 128:34 * 128], ps[:, b, 128:256], [PB[b]], [KTB[0]])
                        else:
                            evac(KT[:, kv, lks[0] * 128:lks[0] * 128 + 512], ps[:, b, 0:512], [PB[b]], [KTB[gi]])
                    for t in range(nt):
                        b = 5 + (t % 2)
                        for dc in range(8):
                            mm(ps[:, b, 0:128], xT[gs][:, dc, t * 128:(t + 1) * 128], wSb[:, dc, 768:896],
                               dc == 0, dc == 7, [XT[gs][t], WSB], [PB[b]])
                        vcopy("vector", VA[:, lks[t], :, 0:64],
                              ps[:, b, 0:128].rearrange("p (k d) -> p k d", k=2), [PB[b]], [VAB[lks[t]]])
                    if gi == 0:
                        for j, lk in enumerate((0, 33)):
                            ts("vector", VA[:, lk, :, :], VA[:, lk, :, :], hv_t[:, j:j + 1], None, ALU.mult, None,
                               [CONST, VAB[lk]], [VAB[lk]])
                        s1_x(gi + 1)
                        return
                    g = gi - 1
                    for c in range(4):
                        b = proj(c * 128)
                        evac(QT[:, c, g * 512:(g + 1) * 512], ps[:, b, 0:512], [PB[b]], [QTB[c][g]])
                    s1_x(gi + 1)
                    for c in range(4):
                        b = proj(896 + c * 128)
                        act(mixedT[:, 4 + c, g * 512:(g + 1) * 512], ps[:, b, 0:512], AF.Silu,
                            [PB[b]], [MX[4 + c][g]])

                s1_xc(0)
                s1_x(0)
                for gi in range(9):
                    s1_group(gi)
            P.enabled = True
            P.barrier()

            with ExitStack() as s2:
                P.enabled = "S2" in phases
                bias_t = T(s2, "bias_t", [128, 2, 3, 512], F32)
                esb = T(s2, "esb", [128, 2, 512], F32)
                tmpS = [T(s2, f"tmpS{i}", [128, 3, 512], F32) for i in range(2)]
                TS_ = [Buf(f"tmps{i}") for i in range(2)]
                PTs = [T(s2, f"PTs{i}", [128, 3, 512], BF16) for i in range(2)]
                PTB = [Buf(f"pts{i}") for i in range(2)]
                den = [T(s2, f"den{i}", [128, 512], F32) for i in range(2)]
                DEN = [Buf(f"den{i}") for i in range(2)]
                tmpo = [T(s2, f"tmpo{i}", [128, 512], F32) for i in range(2)]
                TMO = [Buf(f"tmo{i}") for i in range(2)]
                C2 = Buf("c2")

                def gpos(g):
                    return (g % 2) * 2 + g // 2
                ind16 = T(s2, "ind16", [1, 128], BF16)
                esb16 = T(s2, "esb16", [1, 2, 512], BF16)
                memset("vector", ind16[0:1, 0:64], 0.0, [C2])
                memset("vector", ind16[0:1, 64:128], 1.0, [C2])
                for kv in range(2):
                    dma("sync", bias_t[:, kv, :, :].rearrange("p b c -> p (b c)"), biasT[kv][:, :], (), [C2])
                for kv in range(2):
                    for g in range(4):
                        h = kv * 4 + g
                        act(esb[:, kv, gpos(g) * 128:(gpos(g) + 1) * 128], zeros32[:], AF.Exp, [CONST], [C2],
                            bias=sink_t[:, h:h + 1])
                vcopy("vector", esb16[:], esb[0:1, :, :], [C2], [C2])

                def s2_qk(it):
                    n, kv = it // 2, it % 2
                    sl = it % 2
                    kbufs = sorted(set(0 if lk in (0, 33) else 1 + (lk - 1) // 4 for lk in (n, n + 1, n + 2)))
                    lastmm = None
                    for hf in range(2):
                        for kb in range(3):
                            b = sl * 3 + kb
                            m = mm(ps[:, b, hf * 256:(hf + 1) * 256],
                                   KT[hf * 64:(hf + 1) * 64, kv, (n + kb) * 128:(n + kb + 1) * 128],
                                   QT[hf * 64:(hf + 1) * 64, kv * 2:kv * 2 + 2, n * 128:(n + 1) * 128],
                                   True, True,
                                   [KTB[k] for k in kbufs] + [QTB[kv * 2][n // 4], QTB[kv * 2 + 1][n // 4]], [PB[b]],
                                   force=[lastmm] if (hf == 1 and kb == 0) else [])
                            if hf == 0:
                                lastmm = m

                def s2_bias(it):
                    n, kv = it // 2, it % 2
                    sl = it % 2
                    stt(tmpS[sl][:].rearrange("p a b -> p (a b)"),
                        ps[:, sl * 3:sl * 3 + 3, :].rearrange("p a b -> p (a b)"), SWA_SCALE,
                        bias_t[:, kv, :, :].rearrange("p a b -> p (a b)"),
                        ALU.mult, ALU.add, [PB[sl * 3 + kb] for kb in range(3)] + [C2], [TS_[sl]])

                def s2_exp(it):
                    sl = it % 2
                    act(PTs[sl][:], tmpS[sl][:], AF.Exp, [TS_[sl]], [PTB[sl]])

                def s2_pv(it):
                    n, kv = it // 2, it % 2
                    sl = it % 2
                    ob = 6 + sl
                    for kb in range(3):
                        mm(ps[:, ob, :], VA[:, n + kb, kv, :], PTs[sl][:, kb, :], kb == 0, False,
                           [VAB[n + kb], PTB[sl]], [PB[ob]])
                    mm(ps[:, ob, :], ind16[0:1, :], esb16[0:1, kv, :], False, True, [C2], [PB[ob]])

                def s2_rden(it):
                    sl = it % 2
                    ob = 6 + sl
                    act(den[sl][64:128, :], ps[64:128, ob, :], AF.Ln, [PB[ob]], [DEN[sl]])
                    act(den[sl][64:128, :], den[sl][64:128, :], AF.Exp, [DEN[sl]], [DEN[sl]], scale=-1.0)

                def s2_rest(it):
                    n, kv = it // 2, it % 2
                    sl = it % 2
                    ob = 6 + sl
                    for hf in range(2):
                        rows = slice(hf * 64, (hf + 1) * 64)
                        gc = slice(hf * 256, (hf + 1) * 256)
                        tt("vector", tmpo[sl][rows, gc], ps[0:64, ob, gc], den[sl][64:128, gc], ALU.mult,
                           [PB[ob], DEN[sl]], [TMO[sl]])
                    for g in range(4):
                        h = kv * 4 + g
                        c, hf = 4 + h // 2, h % 2
                        rows = slice(hf * 64, (hf + 1) * 64)
                        dst = mixedT[rows, c, n * 128:(n + 1) * 128]
                        tt("gpsimd", dst, tmpo[sl][rows, gpos(g) * 128:(gpos(g) + 1) * 128], dst, ALU.mult,
                           [TMO[sl], MX[c][n // 4]], [MX[c][n // 4]])

                s2_qk(0)
                s2_bias(0)
                s2_exp(0)
                for it in range(64):
                    if it + 1 < 64:
                        s2_qk(it + 1)
                    if it >= 1:
                        s2_rden(it - 1)
                    if it + 1 < 64:
                        s2_bias(it + 1)
                    if it >= 1:
                        s2_rest(it - 1)
                    if it + 1 < 64:
                        s2_exp(it + 1)
                    s2_pv(it)
                s2_rden(63)
                s2_rest(63)
            P.enabled = True
            P.barrier()

        with ExitStack() as sM:
            cqn = T(sM, "cqn", [128, 2, NQ], BF16)
            CQN = [Buf(f"cqn{g}") for g in range(8)]
            ckvn = T(sM, "ckvn", [128, NK], BF16)
            CKV = [Buf(f"ckv{g}") for g in range(16)]
            KTm = T(sM, "KTm", [128, NK], BF16)
            KNB = [Buf(f"knb{g}") for g in range(16)]
            KRB = [Buf(f"krb{g}") for g in range(16)]
            wuqb = T(sM, "wuqb", [128, 2, 1024], BF16)
            wukvb = T(sM, "wukvb", [128, 1024], BF16)
            WQB = Buf("wqb")
            WKB = Buf("wkb")
            cst = [T(sM, f"cst{i}", [128, 512], F32) for i in range(2)]
            CST = [Buf(f"cst{i}") for i in range(2)]
            t12 = [T(sM, f"t12_{i}", [128, 512], F32) for i in range(2)]
            T12 = [Buf(f"t12_{i}") for i in range(2)]
            csn = {"n": 0}

            def rope(b, s, dst, DST):
                tt("vector", t12[s][64:96, :], ps[96:128, b, :], cst[s][96:128, :], ALU.mult,
                   [PB[b], CST[s]], [T12[s]])
                tt("vector", t12[1 - s][64:96, :], ps[64:96, b, :], cst[s][64:96, :], ALU.mult,
                   [PB[b], CST[s]], [T12[1 - s]])
                tt("gpsimd", dst, t12[1 - s][64:96, :], t12[s][64:96, :], ALU.add, [T12[s], T12[1 - s]], [DST])

            with ExitStack() as sa:
                P.enabled = "A" in phases
                alloc_staging(sa, "a")
                xT, XT = stg["xT"], stg["XT"]
                wAb = T(sa, "wAb", [128, 8, WA_COLS], BF16)
                WAB = Buf("wab")
                sq = T(sa, "sq", [128, 3, 512], F32)
                SQ = [Buf(f"sq{i}") for i in range(3)]
                rb = T(sa, "rb", [128, 2, 512], F32)
                RB = Buf("rb")
                load_weight_chunks(wA, WA_COLS, lambda c: wAb[:, c, :], 8, WAB)
                load_weight_chunks(wuq, 1024, lambda c: wuqb[:, c, :], 2, WQB)
                load_weight_chunks(wukv, 1024, lambda c: wukvb[:, :], 1, WKB)

                xids = {}

                def a_xc(kg):
                    if kg > 15:
                        return
                    src = xq if kg < 8 else xo
                    g = kg % 8
                    srcs = [src[(g * 4 + t) * 128:(g * 4 + t + 1) * 128, :] for t in range(4)]
                    xids[kg] = x_cast(srcs)

                def a_x(kg):
                    if kg > 15:
                        return
                    x_tr(xids[kg], kg % 2)

                def a_group(kg):
                    own = kg < 8
                    gs = kg % 2
                    g = kg % 8
                    xr = XT[gs][0:4]
                    a_xc(kg + 1)

                    def proj(b, c0, M=128):
                        for dc in range(8):
                            mm(ps[0:M, b, :], wAb[:, dc, c0:c0 + M], xT[gs][:, dc, :],
                               dc == 0, dc == 7, xr + [WAB], [PB[b]])

                    proj(4, 256)
                    act(sq[:, 2, :], ps[:, 4, :], AF.Square, [PB[4]], [SQ[2]])
                    mm(ps[:, 6, :], ones32[:], sq[:, 2, :], True, True, [CONST, SQ[2]], [PB[6]])
                    act(rb[:, 1, :], ps[:, 6, :], AF.Ln, [PB[6]], [RB], scale=1.0 / 128.0, bias=eps_t[:, 0:1])
                    act(rb[:, 1, :], rb[:, 1, :], AF.Exp, [RB], [RB], scale=-0.5)
                    if own:
                        proj(2, 0)
                        proj(3, 128)
                        act(sq[:, 0, :], ps[:, 2, :], AF.Square, [PB[2]], [SQ[0]])
                        act(sq[:, 1, :], ps[:, 3, :], AF.Square, [PB[3]], [SQ[1]])
                        mm(ps[:, 5, :], ones32[:], sq[:, 0, :], True, False, [CONST, SQ[0]], [PB[5]])
                        mm(ps[:, 5, :], ones32[:], sq[:, 1, :], False, True, [CONST, SQ[1]], [PB[5]])
                        act(rb[:, 0, :], ps[:, 5, :], AF.Ln, [PB[5]], [RB], scale=1.0 / 256.0, bias=eps_t[:, 0:1])
                        act(rb[:, 0, :], rb[:, 0, :], AF.Exp, [RB], [RB], scale=-0.5)
                        for cc in range(2):
                            stt(cqn[:, cc, g * 512:(g + 1) * 512], ps[:, 2 + cc, :], gq_t[:, cc:cc + 1], rb[:, 0, :],
                                ALU.mult, ALU.mult, [PB[2 + cc], RB, CONST], [CQN[g]])
                    stt(ckvn[:, kg * 512:(kg + 1) * 512], ps[:, 4, :], gkv_t[:, 0:1], rb[:, 1, :],
                        ALU.mult, ALU.mult, [PB[4], RB, CONST], [CKV[kg]])
                    proj(7, 384)
                    s = csn["n"] % 2
                    csn["n"] += 1
                    dma("sync", cst[s][64:128, :], cs[kg, :, :], (), [CST[s]])
                    rope(7, s, KTm[64:96, kg * 512:(kg + 1) * 512], KRB[kg])
                    a_x(kg + 1)
                    if own:
                        for c in range(4):
                            b = (5, 6, 7)[c % 3]
                            proj(b, 512 + c * 128)
                            act(mixedT[:, c, g * 512:(g + 1) * 512], ps[:, b, :], AF.Silu, [PB[b]], [MX[c][g]])

                a_xc(0)
                a_x(0)
                for kg in range(16):
                    a_group(kg)
            P.enabled = True
            P.barrier()

            with ExitStack() as sc:
                P.enabled = "C" in phases
                VAm = T(sc, "VAm", [128, 64, 128], BF16)
                VMB = [Buf(f"vmb{i}") for i in range(8)]
                QTm = T(sc, "QTm", [128, NQ], BF16)
                QMB = [Buf(f"qmb{i}") for i in range(8)]
                PT = [T(sc, f"PT{i}", [128, 3, 512], BF16) for i in range(3)]
                PTB = [Buf(f"ptb{i}") for i in range(3)]
                rden = [T(sc, f"rden{i}", [128, 512], F32) for i in range(2)]
                RDN = [Buf(f"rdn{i}") for i in range(2)]
                tmpo = [T(sc, f"tmpoC{i}", [128, 512], F32) for i in range(2)]
                TMO = [Buf(f"tmoC{i}") for i in range(2)]

                def vo(h):
                    return (0, 64) if h % 2 == 0 else (64, 0)

                def pcopy(out, in_, reads, writes, up):
                    if up:
                        evac(out, in_, reads, writes)
                    else:
                        vcopy("vector", out, in_, reads, writes)

                def prep_k(h, kg, b, up=False):
                    mm(ps[0:64, b, :], wukvb[:, h * 128:h * 128 + 64], ckvn[:, kg * 512:(kg + 1) * 512],
                       True, True, [WKB, CKV[kg]], [PB[b]])
                    pcopy(KTm[0:64, kg * 512:(kg + 1) * 512], ps[0:64, b, :], [PB[b]], [KNB[kg]], up)

                def ones_v(h, vb, b=None):
                    voff, ooff = vo(h)
                    memset("gpsimd", VAm[:, vb * 8:(vb + 1) * 8, ooff:ooff + 64], 1.0, [VMB[vb]])

                def prep_v(h, vb, b, up=False):
                    voff, ooff = vo(h)
                    for j in range(8):
                        kblk = vb * 8 + j
                        mm(ps[:, b, j * 64:(j + 1) * 64], ckvn[:, kblk * 128:(kblk + 1) * 128],
                           wukvb[:, h * 128 + 64:h * 128 + 128], True, True,
                           [WKB, CKV[kblk // 4]], [PB[b]])
                    if up:
                        ones_v(h, vb)
                    pcopy(VAm[:, vb * 8:(vb + 1) * 8, voff:voff + 64],
                          ps[:, b, :].rearrange("p (j d) -> p j d", j=8), [PB[b]], [VMB[vb]], up)

                def cs_dma(h, qg, b=None):
                    s = (h * 8 + qg) % 2
                    dma("sync", cst[s][64:128, :], cs[qg, :, :], (), [CST[s]])

                def prep_q(h, qg, b, up=False):
                    for cc in range(2):
                        mm(ps[:, b, :], wuqb[:, cc, h * 128:(h + 1) * 128], cqn[:, cc, qg * 512:(qg + 1) * 512],
                           cc == 0, cc == 1, [WQB, CQN[qg]], [PB[b]])
                    pcopy(QTm[0:64, qg * 512:(qg + 1) * 512], ps[0:64, b, :], [PB[b]], [QMB[qg]], up)
                    rope(b, (h * 8 + qg) % 2, QTm[64:96, qg * 512:(qg + 1) * 512], QMB[qg])

                for qg in range(8):
                    cs_dma(0, qg)
                    prep_q(0, qg, qg % 6, True)
                for kg in range(16):
                    prep_k(0, kg, kg % 6, True)
                for vb in range(8):
                    prep_v(0, vb, vb % 6, True)

                hooks = {}
                now_hooks = {}
                for h in range(7):
                    for qg in range(8):
                        now_hooks.setdefault((h, qg, 20), []).append((cs_dma, h + 1, qg))
                        hooks.setdefault((h, qg, 63), []).append((prep_q, h + 1, qg))
                    for kg in range(16):
                        hooks.setdefault((h, 7, 4 * kg + 3), []).append((prep_k, h + 1, kg))
                    for vb in range(8):
                        now_hooks.setdefault((h, 7, 8 * vb + 7), []).append((ones_v, h + 1, vb))
                        hooks.setdefault((h, 7, 8 * vb + 7), []).append((prep_v, h + 1, vb))
                pending = []
                tcnt = {"n": 0}

                def fin_part(h, qg, c):
                    voff, ooff = vo(h)
                    s, ob = qg % 2, 6 + qg % 2
                    orow = slice(voff, voff + 64)
                    srow = slice(ooff, ooff + 64)
                    cc_ = slice(c * 128, (c + 1) * 128)
                    recip(rden[s][srow, cc_], ps[srow, ob, cc_], [PB[ob]], [RDN[s]])
                    tt("vector", tmpo[s][orow, cc_], ps[orow, ob, cc_], rden[s][srow, cc_], ALU.mult,
                       [PB[ob], RDN[s]], [TMO[s]])
                    dst = mixedT[orow, h // 2, qg * 512 + c * 128:qg * 512 + (c + 1) * 128]
                    tt("gpsimd", dst, tmpo[s][orow, cc_], dst, ALU.mult,
                       [TMO[s], MX[h // 2][qg]], [MX[h // 2][qg]])

                tiles = [(h, qg, kb) for h in range(8) for qg in range(8) for kb in range(64)]
                glist = []
                tpos = {"i": 0}
                last_prep = {"j": -100}

                def prep_would_fire(groups):
                    pq = [it_[0] for it_ in pending]
                    n_ = tcnt["n"]
                    for g_ in groups:
                        for (h, qg, kb) in g_:
                            if kb == 63:
                                pq += [fin_part] * 4
                            for fn, hh, idx in hooks.get((h, qg, kb), ()):
                                pq.append(fn)
                            n_ += 1
                            if pq and n_ % 3 == 0:
                                if pq.pop(0) is not fin_part:
                                    return True
                    return False

                def form_group(j):
                    if tpos["i"] >= len(tiles):
                        return False
                    assert j == len(glist)
                    n_ = 3
                    if j % 2 == 1:
                        prev = [glist[x] for x in (j - 2, j - 1) if x >= 0]
                        if last_prep["j"] > j - 6 or prep_would_fire(prev):
                            n_ = 2
                    glist.append(tiles[tpos["i"]:tpos["i"] + n_])
                    tpos["i"] += n_
                    return True

                def emit_qk(j):
                    sb0 = 0 if j % 2 == 0 else 3
                    for i, (h, qg, kb) in enumerate(glist[j]):
                        mm(ps[:, sb0 + i, :], KTm[0:96, kb * 128:(kb + 1) * 128],
                           QTm[0:96, qg * 512:(qg + 1) * 512], True, True,
                           [KNB[kb // 4], KRB[kb // 4], QMB[qg]], [PB[sb0 + i]])

                def emit_exp(j):
                    sb0 = 0 if j % 2 == 0 else 3
                    pslot = j % 3
                    ng = len(glist[j])
                    act(PT[pslot][:, 0:ng, :], ps[:, sb0:sb0 + ng, :], AF.Exp,
                        [PB[sb0 + i] for i in range(ng)], [PTB[pslot]], scale=MLA_SCALE)

                def emit_pv(j):
                    pslot = j % 3
                    gt = glist[j]
                    for i, (h, qg, kb) in enumerate(gt):
                        voff, ooff = vo(h)
                        ob = 6 + qg % 2
                        mm(ps[:, ob, :], VAm[:, kb, :], PT[pslot][:, i, :], kb == 0, kb == 63,
                           [VMB[kb // 8], PTB[pslot]], [PB[ob]])
                        for fn, hh, idx in now_hooks.get((h, qg, kb), ()):
                            fn(hh, idx)
                        if kb == 63:
                            for c in range(4):
                                pending.append((fin_part, h, qg, c))
                        for fn, hh, idx in hooks.get((h, qg, kb), ()):
                            pending.append((fn, hh, idx, 5))
                        tcnt["n"] += 1
                        if pending and tcnt["n"] % 3 == 0:
                            it_ = pending.pop(0)
                            if it_[0] is not fin_part:
                                last_prep["j"] = j
                            it_[0](*it_[1:])

                form_group(0)
                emit_qk(0)
                form_group(1)
                emit_qk(1)
                j = 0
                while j < len(glist):
                    emit_exp(j)
                    if form_group(j + 2):
                        emit_qk(j + 2)
                    emit_pv(j)
                    j += 1
                while pending:
                    it_ = pending.pop(0)
                    it_[0](*it_[1:])
            P.enabled = True
            P.barrier()

        with ExitStack() as sd:
            alloc_staging(sd, "d", full=False)
            xstage, XS = stg["xstage"], stg["XS"]
            woutb = T(sd, "woutb", [128, 8, 1024], BF16)
            WOB = Buf("wob")
            lng_t = T(sd, "lng_t", [128, 1024], F32)
            lnb_t = T(sd, "lnb_t", [128, 1024], F32)
            LNC = Buf("lnc")
            NS = 4
            junk = T(sd, "junk", [128, 1024], BF16)
            JNK = Buf("junk")
            yb = [T(sd, f"yb{i}", [128, 1024], F32) for i in range(NS)]
            YB = [Buf(f"yb{i}") for i in range(NS)]
            st6 = [T(sd, f"st6_{i}", [128, 12], F32) for i in range(NS)]
            ST6 = [Buf(f"st6_{i}") for i in range(NS)]
            mv = [T(sd, f"mv{i}", [128, 2], F32) for i in range(NS)]
            MV = [Buf(f"mv{i}") for i in range(NS)]
            rstd = [T(sd, f"rstd{i}", [128, 1], F32) for i in range(NS)]
            RSD = [Buf(f"rsd{i}") for i in range(NS)]
            nmr = [T(sd, f"nmr{i}", [128, 1], F32) for i in range(NS)]
            NMR = [Buf(f"nmr{i}") for i in range(NS)]
            load_weight_chunks(wout, 1024, lambda c: woutb[:, c, :], 8, WOB)
            dma("sync", lng_t[:], lng.partition_broadcast(128), (), [LNC])
            dma("sync", lnb_t[:], lnb.partition_broadcast(128), (), [LNC])
            stores = []
            NT = NQ // 128

            def load_x(t):
                if t < NT:
                    s3 = t % 3
                    dma("sync", xstage[s3][:], xq[t * 128:(t + 1) * 128, :], (), [XS[s3]])

            def d_a(t):
                s, s3 = t % NS, t % 3
                for hh in range(2):
                    b = (t % 2) * 2 + hh
                    for k in range(8):
                        mm(ps[:, b, :], mixedT[:, k, t * 128:(t + 1) * 128], woutb[:, k, hh * 512:(hh + 1) * 512],
                           k == 0, k == 7, [MX[k][t // 4], WOB], [PB[b]])
                    stt(yb[s][:, hh * 512:(hh + 1) * 512], xstage[s3][:, hh * 512:(hh + 1) * 512], ALPHA,
                        ps[:, b, :], ALU.mult, ALU.add, [XS[s3], PB[b]], [YB[s]])
                act(junk[:], yb[s][:], AF.Identity, [YB[s]], [JNK, ST6[s]], accum_out=st6[s][:, 0:1])
                act(junk[:], yb[s][:], AF.Square, [YB[s]], [JNK, ST6[s]], accum_out=st6[s][:, 1:2])

            def d_a2(t):
                s = t % NS
                ts("vector", mv[s][:, 0:1], st6[s][:, 0:1], 1.0 / D_MODEL, None, ALU.mult, None, [ST6[s]], [MV[s]])
                tt("vector", st6[s][:, 2:3], mv[s][:, 0:1], mv[s][:, 0:1], ALU.mult, [MV[s]], [ST6[s]])
                stt(mv[s][:, 1:2], st6[s][:, 1:2], 1.0 / D_MODEL, st6[s][:, 2:3], ALU.mult, ALU.subtract,
                    [ST6[s]], [MV[s]])
                act(rstd[s][:], mv[s][:, 1:2], AF.Sqrt, [MV[s]], [RSD[s]], bias=1e-5)

            def d_b1(t):
                s = t % NS
                recip(rstd[s][:], rstd[s][:], [RSD[s]], [RSD[s]])
                ts("vector", nmr[s][:], mv[s][:, 0:1], rstd[s][:, 0:1], -1.0, ALU.mult, ALU.mult,
                   [MV[s], RSD[s]], [NMR[s]])
                act(yb[s][:], yb[s][:], AF.Identity, [YB[s], RSD[s], NMR[s]], [YB[s]],
                    scale=rstd[s][:, 0:1], bias=nmr[s][:, 0:1])

            def d_b2(t):
                s = t % NS
                tt("vector", yb[s][:], yb[s][:], lng_t[:], ALU.mult, [YB[s], LNC], [YB[s]])
                tt("gpsimd", yb[s][:], yb[s][:], lnb_t[:], ALU.add, [YB[s], LNC], [YB[s]])
                stores.append(dma("sync", y[t * 128:(t + 1) * 128, :], yb[s][:], [YB[s]], []))

            load_x(0)
            load_x(1)
            load_x(2)
            d_a(0)
            d_a2(0)
            d_a(1)
            d_a2(1)
            d_b1(0)
            for t in range(NT):
                load_x(t + 3)
                if t + 2 < NT:
                    d_a(t + 2)
                d_b2(t)
                if t + 1 < NT:
                    d_b1(t + 1)
                if t + 2 < NT:
                    d_a2(t + 2)
            P.op("sync", None, deps=stores)
            P.op("scalar", None, deps=stores)

            with ExitStack() as se:
                sems = {e: se.enter_context(nc.semaphore(f"s_{e}")) for e in Prog.ENGS}
                dma_sems = {e: [se.enter_context(nc.semaphore(f"d_{e}{i}")) for i in range(Prog.DMA_POOL)]
                            for e in Prog.ENGS}
                P.finalize_tokens(sems, dma_sems)
                block = se.enter_context(nc.Block())

                @block.sync
                def _(e):
                    P.emit("sync", e)

                @block.tensor
                def _(e):
                    P.emit("tensor", e)

                @block.vector
                def _(e):
                    P.emit("vector", e)

                @block.scalar
                def _(e):
                    P.emit("scalar", e)

                @block.gpsimd
                def _(e):
                    P.emit("gpsimd", e)
    return nc


_CACHE = {}


def kernel(x, w_in, g_q, g_kv, w_uq, w_ukv, sink, rel_bias, w_out, ln_g, ln_b):
    x = np.asarray(x, np.float32)
    w_in = np.asarray(w_in, np.float32)
    w_uq = np.asarray(w_uq, np.float32)
    w_ukv = np.asarray(w_ukv, np.float32)
    w_out = np.asarray(w_out, np.float32)
    rel_bias = np.asarray(rel_bias, np.float32)
    o_cq, o_ckv, o_kr, o_ga, o_qs, o_ks, o_vs, o_gb = 0, 256, 384, 416, 928, 1440, 1568, 1696
    kr_cols = np.arange(o_kr, o_kr + 32)
    kr_sw = np.concatenate([kr_cols[16:], kr_cols[:16]])
    colsA = np.concatenate([np.arange(o_cq, o_cq + 256), np.arange(o_ckv, o_ckv + 128), kr_cols, kr_sw,
                            kr_cols, kr_sw, np.arange(o_ga, o_ga + 512)])
    k0 = np.arange(o_ks, o_ks + 64)
    k1 = np.arange(o_ks + 64, o_ks + 128)
    colsS = np.concatenate([np.arange(o_qs, o_qs + 512), k0, k0, k1, k1, np.arange(o_vs, o_vs + 128),
                            np.arange(o_gb, o_gb + 512)])
    wA = np.ascontiguousarray(w_in[:, colsA])
    wS = np.ascontiguousarray(w_in[:, colsS])
    cq = []
    for h in range(8):
        base = h * 96
        rope = np.arange(base + 64, base + 96)
        cq.append(np.concatenate([np.arange(base, base + 64), rope, rope[16:], rope[:16]]))
    wuq = np.ascontiguousarray(w_uq[:, np.concatenate(cq)])
    gq = np.ascontiguousarray(np.asarray(g_q, np.float32).reshape(2, 128).T)
    gkv = np.ascontiguousarray(np.asarray(g_kv, np.float32).reshape(128, 1))
    q_loc = np.arange(128)
    k_loc = np.arange(384) - 128
    rel = k_loc[None, :] - q_loc[:, None]
    band = np.abs(rel) <= 128
    bucket = _t5_bucket(rel)
    bias = rel_bias[bucket]
    bias = np.where(band[:, :, None], bias, np.float32(NEG)).astype(np.float32)
    bT = bias.reshape(128, 3, 128, 2, 4).transpose(2, 3, 1, 4, 0)
    bT = bT[:, :, :, [0, 2, 1, 3], :]
    biasT = np.ascontiguousarray(bT.reshape(128, 3072))
    ident = np.eye(128, dtype=np.float32)

    in_maps = []
    for c in range(N_CORES):
        b, half = c // 2, c % 2
        own0 = half * NQ
        oth0 = (1 - half) * NQ
        xq_c = x[b, own0:own0 + NQ]
        xo_c = x[b, oth0:oth0 + NQ]
        xh_c = np.zeros((256, D_MODEL), np.float32)
        hv_c = np.zeros((128, 2), np.float32)
        if own0 - 128 >= 0:
            xh_c[0:128] = x[b, own0 - 128:own0]
            hv_c[:, 0] = 1.0
        if own0 + NQ + 128 <= SEQ:
            xh_c[128:256] = x[b, own0 + NQ:own0 + NQ + 128]
            hv_c[:, 1] = 1.0
        pos = np.concatenate([np.arange(own0, own0 + NQ), np.arange(oth0, oth0 + NQ)])
        in_maps.append({
            "xq": np.ascontiguousarray(xq_c), "xo": np.ascontiguousarray(xo_c), "xh": xh_c, "hv": hv_c,
            "wukv": w_ukv, "gq": gq, "gkv": gkv,
            "cs": _rope_tables(pos), "sink": np.asarray(sink, np.float32),
            **{f"wA{i}": wA[i * 128:(i + 1) * 128] for i in range(8)},
            **{f"wS{i}": wS[i * 128:(i + 1) * 128] for i in range(8)},
            **{f"wuq{i}": wuq[i * 128:(i + 1) * 128] for i in range(2)},
            **{f"wout{i}": w_out[i * 128:(i + 1) * 128] for i in range(8)},
            **{f"biasT{kv}": np.ascontiguousarray(biasT[:, kv * 1536:(kv + 1) * 1536]) for kv in range(2)},
            "lng": np.asarray(ln_g, np.float32), "lnb": np.asarray(ln_b, np.float32), "ident": ident,
        })
    if "nc" not in _CACHE:
        _CACHE["nc"] = build_program()
    res = run_bass_kernel_spmd(_CACHE["nc"], in_maps, core_ids=list(range(N_CORES)))
    out = np.empty((BATCH, SEQ, D_MODEL), np.float32)
    for c in range(N_CORES):
        b, half = c // 2, c % 2
        out[b, half * NQ:(half + 1) * NQ] = res.results[c]["y"]
    return out
```

```python
import math
from contextlib import ExitStack

import numpy as np
import concourse.bass as bass
import concourse.mybir as mybir
from concourse.bass_utils import run_bass_kernel_spmd

F32 = mybir.dt.float32
BF16 = mybir.dt.bfloat16
AF = mybir.ActivationFunctionType
ALU = mybir.AluOpType

D_MODEL = 1024
BATCH = 4
SEQ = 8192
NQ = 4096
NK = 8192
MLA_SCALE = 1.0 / math.sqrt(96.0)
SWA_SCALE = 0.125
ALPHA = 2.0 ** 0.25
N_CORES = 8
NEG = -30000.0

WA_COLS = 1024
WS_COLS = 1408


class Buf:
    __slots__ = ("name", "w", "r")

    def __init__(self, name):
        self.name = name
        self.w = None
        self.r = []


class Op:
    __slots__ = ("eng", "fn", "deps", "users", "is_dma", "sem", "val", "qidx")


class Prog:
    ENGS = ("tensor", "vector", "scalar", "gpsimd", "sync")
    DMA_POOL = 8

    def __init__(self):
        self.ops = {e: [] for e in self.ENGS}
        self.ndma = {e: 0 for e in self.ENGS}
        self.dma_last = {}
        self.last = {e: None for e in self.ENGS}
        self.enabled = True

    def op(self, eng, fn, reads=(), writes=(), deps=(), dma=False, force=()):
        if not self.enabled:
            return None
        o = Op()
        o.eng, o.fn, o.is_dma, o.users = eng, fn, dma, False
        o.sem = o.val = o.qidx = None
        d = set(x for x in deps if x is not None)
        for b in tuple(reads) + tuple(writes):
            if b.w is not None:
                d.add(b.w)
        for b in writes:
            d.update(b.r)
        d.discard(o)
        if dma:
            o.qidx = self.ndma[eng]
            self.ndma[eng] += 1
            prev = self.dma_last.get((eng, o.qidx % self.DMA_POOL))
            if prev is not None:
                d.add(prev)
            self.dma_last[(eng, o.qidx % self.DMA_POOL)] = o
            o.users = True
        o.deps = [x for x in d if not (x.eng == "tensor" and eng == "tensor" and not x.is_dma)]
        o.deps += [x for x in force if x is not None and x not in o.deps]
        for x in o.deps:
            x.users = True
        for b in reads:
            b.r.append(o)
        for b in writes:
            b.w = o
            b.r = []
        self.ops[eng].append(o)
        if fn is not None:
            self.last[eng] = o
        return o

    def barrier(self):
        lasts = [self.last[e] for e in self.ENGS if self.last[e] is not None]
        lasts += list(self.dma_last.values())
        for e in self.ENGS:
            self.op(e, None, deps=lasts)

    def finalize_tokens(self, sems, dma_sems):
        for e in self.ENGS:
            cnt = 0
            for o in self.ops[e]:
                if o.is_dma:
                    o.sem = dma_sems[e][o.qidx % self.DMA_POOL]
                    o.val = 16 * (o.qidx // self.DMA_POOL + 1)
                elif o.users:
                    cnt += 1
                    o.sem = sems[e]
                    o.val = cnt

    def emit(self, eng_name, e):
        waited = {}
        for o in self.ops[eng_name]:
            need = {}
            for x in o.deps:
                k = id(x.sem)
                if k not in need or need[k][1] < x.val:
                    need[k] = (x.sem, x.val)
            for k, (sem, val) in need.items():
                if waited.get(k, 0) >= val:
                    continue
                e.wait_ge(sem, val)
                waited[k] = val
            if o.fn is None:
                continue
            inst = o.fn(e)
            if o.is_dma:
                inst.then_inc(o.sem, 16)
            elif o.users:
                inst.then_inc(o.sem, 1)


def _t5_bucket(rel):
    half = 16
    ret = np.where(rel > 0, half, 0)
    n = np.abs(rel)
    max_exact = half // 2
    large = max_exact + (np.log(np.maximum(n, 1).astype(np.float32) / max_exact)
                         / np.log(128 / max_exact) * (half - max_exact)).astype(np.int32)
    large = np.minimum(large, half - 1)
    return (ret + np.where(n < max_exact, n, large)).astype(np.int32)


def _rope_tables(positions):
    inv_freq = 10000.0 ** (-np.arange(0, 32, 2, dtype=np.float64) / 32.0)
    ang = positions.astype(np.float64)[:, None] * inv_freq[None, :]
    cos = np.cos(ang).astype(np.float32)
    sin = np.sin(ang).astype(np.float32)
    cos2 = np.concatenate([cos, cos], axis=1)
    sin2 = np.concatenate([-sin, sin], axis=1)
    tab = np.concatenate([cos2, sin2], axis=1)
    ng = positions.shape[0] // 512
    return np.ascontiguousarray(tab.reshape(ng, 512, 64).transpose(0, 2, 1))


def build_program(phases=("S1", "S2", "A", "C", "D")):
    nc = bass.Bass("TRN2", target_bir_lowering=False)
    dt = nc.dram_tensor
    xq = dt("xq", [NQ, D_MODEL], F32, kind="ExternalInput").ap()
    xo = dt("xo", [NQ, D_MODEL], F32, kind="ExternalInput").ap()
    xh = dt("xh", [256, D_MODEL], F32, kind="ExternalInput").ap()
    hv = dt("hv", [128, 2], F32, kind="ExternalInput").ap()
    wA = [dt(f"wA{c}", [128, WA_COLS], F32, kind="ExternalInput").ap() for c in range(8)]
    wS = [dt(f"wS{c}", [128, WS_COLS], F32, kind="ExternalInput").ap() for c in range(8)]
    wuq = [dt(f"wuq{c}", [128, 1024], F32, kind="ExternalInput").ap() for c in range(2)]
    wukv = [dt("wukv", [128, 1024], F32, kind="ExternalInput").ap()]
    wout = [dt(f"wout{c}", [128, 1024], F32, kind="ExternalInput").ap() for c in range(8)]
    gq = dt("gq", [128, 2], F32, kind="ExternalInput").ap()
    gkv = dt("gkv", [128, 1], F32, kind="ExternalInput").ap()
    cs = dt("cs", [16, 64, 512], F32, kind="ExternalInput").ap()
    biasT = [dt(f"biasT{kv}", [128, 1536], F32, kind="ExternalInput").ap() for kv in range(2)]
    sink = dt("sink", [8], F32, kind="ExternalInput").ap()
    lng = dt("lng", [1024], F32, kind="ExternalInput").ap()
    lnb = dt("lnb", [1024], F32, kind="ExternalInput").ap()
    ident = dt("ident", [128, 128], F32, kind="ExternalInput").ap()
    y = dt("y", [NQ, D_MODEL], F32, kind="ExternalOutput").ap()

    P = Prog()
    cnt = {"ev": 0}

    def mm(out, lhsT, rhs, start, stop, reads, writes, force=()):
        return P.op("tensor", lambda e: e.matmul(out, lhsT=lhsT, rhs=rhs, start=start, stop=stop),
                    reads, writes, force=force)

    def tr(out, in_, idn, reads, writes):
        return P.op("tensor", lambda e: e.transpose(out=out, in_=in_, identity=idn), reads, writes)

    def act(out, in_, func, reads, writes, scale=None, bias=None, accum_out=None):
        kw = {}
        if scale is not None:
            kw["scale"] = scale
        if bias is not None:
            kw["bias"] = bias
        if accum_out is not None:
            kw["accum_out"] = accum_out
        return P.op("scalar", lambda e: e.activation(out=out, in_=in_, func=func, **kw), reads, writes)

    def vcopy(eng, out, in_, reads, writes):
        if eng == "scalar":
            return P.op("scalar", lambda e: e.copy(out=out, in_=in_), reads, writes)
        return P.op(eng, lambda e: e.tensor_copy(out=out, in_=in_), reads, writes)

    def evac(out, in_, reads, writes):
        cnt["ev"] += 1
        return vcopy("vector" if cnt["ev"] % 2 else "scalar", out, in_, reads, writes)

    def tt(eng, out, in0, in1, op, reads, writes):
        return P.op(eng, lambda e: e.tensor_tensor(out=out, in0=in0, in1=in1, op=op), reads, writes)

    def stt(out, in0, scalar, in1, op0, op1, reads, writes):
        return P.op("vector", lambda e: e.scalar_tensor_tensor(out=out, in0=in0, scalar=scalar, in1=in1,
                                                                op0=op0, op1=op1), reads, writes)

    def ts(eng, out, in0, s1, s2, op0, op1, reads, writes):
        if op1 is None:
            return P.op(eng, lambda e: e.tensor_scalar(out=out, in0=in0, scalar1=s1, scalar2=None, op0=op0),
                        reads, writes)
        return P.op(eng, lambda e: e.tensor_scalar(out=out, in0=in0, scalar1=s1, scalar2=s2, op0=op0, op1=op1),
                    reads, writes)

    def recip(out, in_, reads, writes):
        return P.op("vector", lambda e: e.reciprocal(out=out, in_=in_), reads, writes)

    def memset(eng, ap, val, writes):
        return P.op(eng, lambda e: e.memset(ap, val), (), writes)

    def dma(eng, out, in_, reads, writes):
        return P.op(eng, lambda e: e.dma_start(out=out, in_=in_), reads, writes, dma=True)

    with ExitStack() as top:
        def T(es, name, shape, dtype):
            return es.enter_context(nc.sbuf_tensor(name, shape, dtype))

        ps = top.enter_context(nc.psum_tensor("ps", [128, 8, 512], F32))
        psb = ps.bitcast(BF16)
        PB = [Buf(f"pb{i}") for i in range(8)]

        mixedT = T(top, "mixedT", [128, 8, NQ], BF16)
        MX = [[Buf(f"mx{c}_{g}") for g in range(8)] for c in range(8)]
        identf = T(top, "identf", [128, 128], F32)
        identb = T(top, "identb", [128, 128], BF16)
        ones32 = T(top, "ones32", [128, 128], F32)
        zeros32 = T(top, "zeros32", [128, 128], F32)
        gq_t = T(top, "gq_t", [128, 2], F32)
        gkv_t = T(top, "gkv_t", [128, 1], F32)
        hv_t = T(top, "hv_t", [128, 2], F32)
        sink_t = T(top, "sink_t", [128, 8], F32)
        CONST = Buf("const")
        stg = {}

        def alloc_staging(es, tag, full=True):
            stg["xstage"] = [T(es, f"xstage{tag}{i}", [128, 1024], F32) for i in range(3)]
            stg["XS"] = [Buf(f"xs{i}") for i in range(3)]
            stg["wstage"] = [T(es, f"wstage{tag}{i}", [128, WS_COLS], F32) for i in range(2)]
            stg["WSTG"] = [Buf(f"wstg{i}") for i in range(2)]
            if full:
                stg["xbt"] = [T(es, f"xbt{tag}{i}", [128, 1024], BF16) for i in range(4)]
                stg["XB"] = [Buf(f"xb{i}") for i in range(4)]
                stg["xT"] = [T(es, f"xT{tag}{i}", [128, 8, 512], BF16) for i in range(2)]
                stg["XT"] = [[Buf(f"xt{i}_{t}") for t in range(4)] for i in range(2)]

        dma("sync", identf[:], ident[:, :], (), [CONST])
        dma("sync", gq_t[:], gq[:, :], (), [CONST])
        dma("sync", gkv_t[:], gkv[:, :], (), [CONST])
        dma("sync", hv_t[:], hv[:, :], (), [CONST])
        dma("sync", sink_t[:], sink.partition_broadcast(128), (), [CONST])
        vcopy("vector", identb[:], identf[:], [CONST], [CONST])
        memset("vector", ones32[:], 1.0, [CONST])
        memset("vector", zeros32[:], 0.0, [CONST])
        eps_t = T(top, "eps_t", [128, 1], F32)
        memset("vector", eps_t[:], 1e-6, [CONST])

        xcount = {"n": 0, "g": 0}

        def load_weight_chunks(src, ncols, dst_fn, nchunks, WB):
            wstage, WSTG = stg["wstage"], stg["WSTG"]
            for c in range(nchunks):
                s = c % 2
                dma("sync", wstage[s][:, 0:ncols], src[c][:, :], (), [WSTG[s]])
                vcopy("scalar" if c % 2 == 0 else "vector", dst_fn(c), wstage[s][:, 0:ncols], [WSTG[s]], [WB])

        def x_cast(srcs):
            xstage, XS, xbt, XB = (stg[k] for k in ("xstage", "XS", "xbt", "XB"))
            ids = []
            for src in srcs:
                i = xcount["n"]
                xcount["n"] += 1
                s3, s4 = i % 3, i % 4
                dma("sync", xstage[s3][:], src, (), [XS[s3]])
                vcopy("vector" if i % 2 else "scalar", xbt[s4][:], xstage[s3][:], [XS[s3]], [XB[s4]])
                ids.append(i)
            return ids

        def x_tr(ids, gs):
            xbt, XB, xT, XT = (stg[k] for k in ("xbt", "XB", "xT", "XT"))
            for t, i in enumerate(ids):
                s2, s4 = i % 2, i % 4
                bank = s2
                for c in range(8):
                    tr(psb[:, bank, c * 128:(c + 1) * 128], xbt[s4][:, c * 128:(c + 1) * 128], identb[:],
                       [XB[s4], CONST], [PB[bank]])
                evac(xT[gs][:, :, t * 128:(t + 1) * 128],
                     psb[:, bank, 0:1024].rearrange("p (c k) -> p c k", c=8),
                     [PB[bank]], [XT[gs][t]])

        with ExitStack() as sS:
            QT = T(sS, "QT", [128, 4, NQ], BF16)
            QTB = [[Buf(f"qt{c}_{g}") for g in range(8)] for c in range(4)]
            KT = T(sS, "KT", [128, 2, 34 * 128], BF16)
            KTB = [Buf(f"kt{g}") for g in range(9)]
            VA = T(sS, "VA", [128, 34, 2, 128], BF16)
            VAB = [Buf(f"va{lk}") for lk in range(34)]
            for lk in range(34):
                memset("gpsimd", VA[:, lk, :, 64:128], 1.0, [VAB[lk]])

            with ExitStack() as s1:
                P.enabled = "S1" in phases
                alloc_staging(s1, "s")
                xT, XT = stg["xT"], stg["XT"]
                wSb = T(s1, "wSb", [128, 8, WS_COLS], BF16)
                WSB = Buf("wsb")
                load_weight_chunks(wS, WS_COLS, lambda c: wSb[:, c, :], 8, WSB)

                xids = {}

                def s1_xc(gi):
                    if gi > 8:
                        return
                    if gi == 0:
                        srcs = [xh[0:128, :], xh[128:256, :]]
                    else:
                        g = gi - 1
                        srcs = [xq[(g * 4 + t) * 128:(g * 4 + t + 1) * 128, :] for t in range(4)]
                    xids[gi] = x_cast(srcs)

                def s1_x(gi):
                    if gi > 8:
                        return
                    x_tr(xids[gi], gi % 2)

                def s1_group(gi):
                    gs = gi % 2
                    if gi == 0:
                        lks = [0, 33]
                    else:
                        g = gi - 1
                        lks = [1 + g * 4 + t for t in range(4)]
                    nt = len(lks)
                    Tn = nt * 128
                    xr = XT[gs][0:nt]
                    s1_xc(gi + 1)
                    pj = [2, 3, 4]
                    pjc = {"i": 0}

                    def proj(c0, M=128):
                        b = pj[pjc["i"] % 3]
                        pjc["i"] += 1
                        for dc in range(8):
                            mm(ps[0:M, b, 0:Tn], wSb[:, dc, c0:c0 + M], xT[gs][:, dc, 0:Tn],
                               dc == 0, dc == 7, xr + [WSB], [PB[b]])
                        return b

                    for kv in range(2):
                        b = proj(512 + kv * 128)
                        if gi == 0:
                            evac(KT[:, kv, 0:128], ps[:, b, 0:128], [PB[b]], [KTB[0]])
                            evac(KT[:, kv, 33 * 128:34 * 128], ps[:, b, 128:256], [PB[b]], [KTB[0]])
                        else:
                            evac(KT[:, kv, lks[0] * 128:lks[0] * 128 + 512], ps[:, b, 0:512], [PB[b]], [KTB[gi]])
                    for t in range(nt):
                        b = 5 + (t % 2)
                        for dc in range(8):
                            mm(ps[:, b, 0:128], xT[gs][:, dc, t * 128:(t + 1) * 128], wSb[:, dc, 768:896],
                               dc == 0, dc == 7, [XT[gs][t], WSB], [PB[b]])
                        vcopy("vector", VA[:, lks[t], :, 0:64],
                              ps[:, b, 0:128].rearrange("p (k d) -> p k d", k=2), [PB[b]], [VAB[lks[t]]])
                    if gi == 0:
                        for j, lk in enumerate((0, 33)):
                            ts("vector", VA[:, lk, :, :], VA[:, lk, :, :], hv_t[:, j:j + 1], None, ALU.mult, None,
                               [CONST, VAB[lk]], [VAB[lk]])
                        s1_x(gi + 1)
                        return
                    g = gi - 1
                    for c in range(4):
                        b = proj(c * 128)
                        evac(QT[:, c, g * 512:(g + 1) * 512], ps[:, b, 0:512], [PB[b]], [QTB[c][g]])
                    s1_x(gi + 1)
                    for c in range(4):
                        b = proj(896 + c * 128)
                        act(mixedT[:, 4 + c, g * 512:(g + 1) * 512], ps[:, b, 0:512], AF.Silu,
                            [PB[b]], [MX[4 + c][g]])

                s1_xc(0)
                s1_x(0)
                for gi in range(9):
                    s1_group(gi)
            P.enabled = True
            P.barrier()

            with ExitStack() as s2:
                P.enabled = "S2" in phases
                bias_t = T(s2, "bias_t", [128, 2, 3, 512], F32)
                esb = T(s2, "esb", [128, 2, 512], F32)
                tmpS = [T(s2, f"tmpS{i}", [128, 3, 512], F32) for i in range(2)]
                TS_ = [Buf(f"tmps{i}") for i in range(2)]
                PTs = [T(s2, f"PTs{i}", [128, 3, 512], BF16) for i in range(2)]
                PTB = [Buf(f"pts{i}") for i in range(2)]
                den = [T(s2, f"den{i}", [128, 512], F32) for i in range(2)]
                DEN = [Buf(f"den{i}") for i in range(2)]
                tmpo = [T(s2, f"tmpo{i}", [128, 512], F32) for i in range(2)]
                TMO = [Buf(f"tmo{i}") for i in range(2)]
                C2 = Buf("c2")

                def gpos(g):
                    return (g % 2) * 2 + g // 2
                ind16 = T(s2, "ind16", [1, 128], BF16)
                esb16 = T(s2, "esb16", [1, 2, 512], BF16)
                memset("vector", ind16[0:1, 0:64], 0.0, [C2])
                memset("vector", ind16[0:1, 64:128], 1.0, [C2])
                for kv in range(2):
                    dma("sync", bias_t[:, kv, :, :].rearrange("p b c -> p (b c)"), biasT[kv][:, :], (), [C2])
                for kv in range(2):
                    for g in range(4):
                        h = kv * 4 + g
                        act(esb[:, kv, gpos(g) * 128:(gpos(g) + 1) * 128], zeros32[:], AF.Exp, [CONST], [C2],
                            bias=sink_t[:, h:h + 1])
                vcopy("vector", esb16[:], esb[0:1, :, :], [C2], [C2])

                def s2_qk(it):
                    n, kv = it // 2, it % 2
                    sl = it % 2
                    kbufs = sorted(set(0 if lk in (0, 33) else 1 + (lk - 1) // 4 for lk in (n, n + 1, n + 2)))
                    lastmm = None
                    for hf in range(2):
                        for kb in range(3):
                            b = sl * 3 + kb
                            m = mm(ps[:, b, hf * 256:(hf + 1) * 256],
                                   KT[hf * 64:(hf + 1) * 64, kv, (n + kb) * 128:(n + kb + 1) * 128],
                                   QT[hf * 64:(hf + 1) * 64, kv * 2:kv * 2 + 2, n * 128:(n + 1) * 128],
                                   True, True,
                                   [KTB[k] for k in kbufs] + [QTB[kv * 2][n // 4], QTB[kv * 2 + 1][n // 4]], [PB[b]],
                                   force=[lastmm] if (hf == 1 and kb == 0) else [])
                            if hf == 0:
                                lastmm = m

                def s2_bias(it):
                    n, kv = it // 2, it % 2
                    sl = it % 2
                    stt(tmpS[sl][:].rearrange("p a b -> p (a b)"),
                        ps[:, sl * 3:sl * 3 + 3, :].rearrange("p a b -> p (a b)"), SWA_SCALE,
                        bias_t[:, kv, :, :].rearrange("p a b -> p (a b)"),
                        ALU.mult, ALU.add, [PB[sl * 3 + kb] for kb in range(3)] + [C2], [TS_[sl]])

                def s2_exp(it):
                    sl = it % 2
                    act(PTs[sl][:], tmpS[sl][:], AF.Exp, [TS_[sl]], [PTB[sl]])

                def s2_pv(it):
                    n, kv = it // 2, it % 2
                    sl = it % 2
                    ob = 6 + sl
                    for kb in range(3):
                        mm(ps[:, ob, :], VA[:, n + kb, kv, :], PTs[sl][:, kb, :], kb == 0, False,
                           [VAB[n + kb], PTB[sl]], [PB[ob]])
                    mm(ps[:, ob, :], ind16[0:1, :], esb16[0:1, kv, :], False, True, [C2], [PB[ob]])

                def s2_rden(it):
                    sl = it % 2
                    ob = 6 + sl
                    act(den[sl][64:128, :], ps[64:128, ob, :], AF.Ln, [PB[ob]], [DEN[sl]])
                    act(den[sl][64:128, :], den[sl][64:128, :], AF.Exp, [DEN[sl]], [DEN[sl]], scale=-1.0)

                def s2_rest(it):
                    n, kv = it // 2, it % 2
                    sl = it % 2
                    ob = 6 + sl
                    for hf in range(2):
                        rows = slice(hf * 64, (hf + 1) * 64)
                        gc = slice(hf * 256, (hf + 1) * 256)
                        tt("vector", tmpo[sl][rows, gc], ps[0:64, ob, gc], den[sl][64:128, gc], ALU.mult,
                           [PB[ob], DEN[sl]], [TMO[sl]])
                    for g in range(4):
                        h = kv * 4 + g
                        c, hf = 4 + h // 2, h % 2
                        rows = slice(hf * 64, (hf + 1) * 64)
                        dst = mixedT[rows, c, n * 128:(n + 1) * 128]
                        tt("gpsimd", dst, tmpo[sl][rows, gpos(g) * 128:(gpos(g) + 1) * 128], dst, ALU.mult,
                           [TMO[sl], MX[c][n // 4]], [MX[c][n // 4]])

                s2_qk(0)
                s2_bias(0)
                s2_exp(0)
                for it in range(64):
                    if it + 1 < 64:
                        s2_qk(it + 1)
                    if it >= 1:
                        s2_rden(it - 1)
                    if it + 1 < 64:
                        s2_bias(it + 1)
                    if it >= 1:
                        s2_rest(it - 1)
                    if it + 1 < 64:
                        s2_exp(it + 1)
                    s2_pv(it)
                s2_rden(63)
                s2_rest(63)
            P.enabled = True
            P.barrier()

        with ExitStack() as sM:
            cqn = T(sM, "cqn", [128, 2, NQ], BF16)
            CQN = [Buf(f"cqn{g}") for g in range(8)]
            ckvn = T(sM, "ckvn", [128, NK], BF16)
            CKV = [Buf(f"ckv{g}") for g in range(16)]
            KTm = T(sM, "KTm", [128, NK], BF16)
            KNB = [Buf(f"knb{g}") for g in range(16)]
            KRB = [Buf(f"krb{g}") for g in range(16)]
            wuqb = T(sM, "wuqb", [128, 2, 1024], BF16)
            wukvb = T(sM, "wukvb", [128, 1024], BF16)
            WQB = Buf("wqb")
            WKB = Buf("wkb")
            cst = [T(sM, f"cst{i}", [128, 512], F32) for i in range(2)]
            CST = [Buf(f"cst{i}") for i in range(2)]
            t12 = [T(sM, f"t12_{i}", [128, 512], F32) for i in range(2)]
            T12 = [Buf(f"t12_{i}") for i in range(2)]
            csn = {"n": 0}

            def rope(b, s, dst, DST):
                tt("vector", t12[s][64:96, :], ps[96:128, b, :], cst[s][96:128, :], ALU.mult,
                   [PB[b], CST[s]], [T12[s]])
                tt("vector", t12[1 - s][64:96, :], ps[64:96, b, :], cst[s][64:96, :], ALU.mult,
                   [PB[b], CST[s]], [T12[1 - s]])
                tt("gpsimd", dst, t12[1 - s][64:96, :], t12[s][64:96, :], ALU.add, [T12[s], T12[1 - s]], [DST])

            with ExitStack() as sa:
                P.enabled = "A" in phases
                alloc_staging(sa, "a")
                xT, XT = stg["xT"], stg["XT"]
                wAb = T(sa, "wAb", [128, 8, WA_COLS], BF16)
                WAB = Buf("wab")
                sq = T(sa, "sq", [128, 3, 512], F32)
                SQ = [Buf(f"sq{i}") for i in range(3)]
                rb = T(sa, "rb", [128, 2, 512], F32)
                RB = Buf("rb")
                load_weight_chunks(wA, WA_COLS, lambda c: wAb[:, c, :], 8, WAB)
                load_weight_chunks(wuq, 1024, lambda c: wuqb[:, c, :], 2, WQB)
                load_weight_chunks(wukv, 1024, lambda c: wukvb[:, :], 1, WKB)

                xids = {}

                def a_xc(kg):
                    if kg > 15:
                        return
                    src = xq if kg < 8 else xo
                    g = kg % 8
                    srcs = [src[(g * 4 + t) * 128:(g * 4 + t + 1) * 128, :] for t in range(4)]
                    xids[kg] = x_cast(srcs)

                def a_x(kg):
                    if kg > 15:
                        return
                    x_tr(xids[kg], kg % 2)

                def a_group(kg):
                    own = kg < 8
                    gs = kg % 2
                    g = kg % 8
                    xr = XT[gs][0:4]
                    a_xc(kg + 1)

                    def proj(b, c0, M=128):
                        for dc in range(8):
                            mm(ps[0:M, b, :], wAb[:, dc, c0:c0 + M], xT[gs][:, dc, :],
                               dc == 0, dc == 7, xr + [WAB], [PB[b]])

                    proj(4, 256)
                    act(sq[:, 2, :], ps[:, 4, :], AF.Square, [PB[4]], [SQ[2]])
                    mm(ps[:, 6, :], ones32[:], sq[:, 2, :], True, True, [CONST, SQ[2]], [PB[6]])
                    act(rb[:, 1, :], ps[:, 6, :], AF.Ln, [PB[6]], [RB], scale=1.0 / 128.0, bias=eps_t[:, 0:1])
                    act(rb[:, 1, :], rb[:, 1, :], AF.Exp, [RB], [RB], scale=-0.5)
                    if own:
                        proj(2, 0)
                        proj(3, 128)
                        act(sq[:, 0, :], ps[:, 2, :], AF.Square, [PB[2]], [SQ[0]])
                        act(sq[:, 1, :], ps[:, 3, :], AF.Square, [PB[3]], [SQ[1]])
                        mm(ps[:, 5, :], ones32[:], sq[:, 0, :], True, False, [CONST, SQ[0]], [PB[5]])
                        mm(ps[:, 5, :], ones32[:], sq[:, 1, :], False, True, [CONST, SQ[1]], [PB[5]])
                        act(rb[:, 0, :], ps[:, 5, :], AF.Ln, [PB[5]], [RB], scale=1.0 / 256.0, bias=eps_t[:, 0:1])
                        act(rb[:, 0, :], rb[:, 0, :], AF.Exp, [RB], [RB], scale=-0.5)
                        for cc in range(2):
                            stt(cqn[:, cc, g * 512:(g + 1) * 512], ps[:, 2 + cc, :], gq_t[:, cc:cc + 1], rb[:, 0, :],
                                ALU.mult, ALU.mult, [PB[2 + cc], RB, CONST], [CQN[g]])
                    stt(ckvn[:, kg * 512:(kg + 1) * 512], ps[:, 4, :], gkv_t[:, 0:1], rb[:, 1, :],
                        ALU.mult, ALU.mult, [PB[4], RB, CONST], [CKV[kg]])
                    proj(7, 384)
                    s = csn["n"] % 2
                    csn["n"] += 1
                    dma("sync", cst[s][64:128, :], cs[kg, :, :], (), [CST[s]])
                    rope(7, s, KTm[64:96, kg * 512:(kg + 1) * 512], KRB[kg])
                    a_x(kg + 1)
                    if own:
                        for c in range(4):
                            b = (5, 6, 7)[c % 3]
                            proj(b, 512 + c * 128)
                            act(mixedT[:, c, g * 512:(g + 1) * 512], ps[:, b, :], AF.Silu, [PB[b]], [MX[c][g]])

                a_xc(0)
                a_x(0)
                for kg in range(16):
                    a_group(kg)
            P.enabled = True
            P.barrier()

            with ExitStack() as sc:
                P.enabled = "C" in phases
                VAm = T(sc, "VAm", [128, 64, 128], BF16)
                VMB = [Buf(f"vmb{i}") for i in range(8)]
                QTm = T(sc, "QTm", [128, NQ], BF16)
                QMB = [Buf(f"qmb{i}") for i in range(8)]
                PT = [T(sc, f"PT{i}", [128, 3, 512], BF16) for i in range(3)]
                PTB = [Buf(f"ptb{i}") for i in range(3)]
                rden = [T(sc, f"rden{i}", [128, 512], F32) for i in range(2)]
                RDN = [Buf(f"rdn{i}") for i in range(2)]
                tmpo = [T(sc, f"tmpoC{i}", [128, 512], F32) for i in range(2)]
                TMO = [Buf(f"tmoC{i}") for i in range(2)]

                def vo(h):
                    return (0, 64) if h % 2 == 0 else (64, 0)

                def pcopy(out, in_, reads, writes, up):
                    if up:
                        evac(out, in_, reads, writes)
                    else:
                        vcopy("vector", out, in_, reads, writes)

                def prep_k(h, kg, b, up=False):
                    mm(ps[0:64, b, :], wukvb[:, h * 128:h * 128 + 64], ckvn[:, kg * 512:(kg + 1) * 512],
                       True, True, [WKB, CKV[kg]], [PB[b]])
                    pcopy(KTm[0:64, kg * 512:(kg + 1) * 512], ps[0:64, b, :], [PB[b]], [KNB[kg]], up)

                def ones_v(h, vb, b=None):
                    voff, ooff = vo(h)
                    memset("gpsimd", VAm[:, vb * 8:(vb + 1) * 8, ooff:ooff + 64], 1.0, [VMB[vb]])

                def prep_v(h, vb, b, up=False):
                    voff, ooff = vo(h)
                    for j in range(8):
                        kblk = vb * 8 + j
                        mm(ps[:, b, j * 64:(j + 1) * 64], ckvn[:, kblk * 128:(kblk + 1) * 128],
                           wukvb[:, h * 128 + 64:h * 128 + 128], True, True,
                           [WKB, CKV[kblk // 4]], [PB[b]])
                    if up:
                        ones_v(h, vb)
                    pcopy(VAm[:, vb * 8:(vb + 1) * 8, voff:voff + 64],
                          ps[:, b, :].rearrange("p (j d) -> p j d", j=8), [PB[b]], [VMB[vb]], up)

                def cs_dma(h, qg, b=None):
                    s = (h * 8 + qg) % 2
                    dma("sync", cst[s][64:128, :], cs[qg, :, :], (), [CST[s]])

                def prep_q(h, qg, b, up=False):
                    for cc in range(2):
                        mm(ps[:, b, :], wuqb[:, cc, h * 128:(h + 1) * 128], cqn[:, cc, qg * 512:(qg + 1) * 512],
                           cc == 0, cc == 1, [WQB, CQN[qg]], [PB[b]])
                    pcopy(QTm[0:64, qg * 512:(qg + 1) * 512], ps[0:64, b, :], [PB[b]], [QMB[qg]], up)
                    rope(b, (h * 8 + qg) % 2, QTm[64:96, qg * 512:(qg + 1) * 512], QMB[qg])

                for qg in range(8):
                    cs_dma(0, qg)
                    prep_q(0, qg, qg % 6, True)
                for kg in range(16):
                    prep_k(0, kg, kg % 6, True)
                for vb in range(8):
                    prep_v(0, vb, vb % 6, True)

                hooks = {}
                now_hooks = {}
                for h in range(7):
                    for qg in range(8):
                        now_hooks.setdefault((h, qg, 20), []).append((cs_dma, h + 1, qg))
                        hooks.setdefault((h, qg, 63), []).append((prep_q, h + 1, qg))
                    for kg in range(16):
                        hooks.setdefault((h, 7, 4 * kg + 3), []).append((prep_k, h + 1, kg))
                    for vb in range(8):
                        now_hooks.setdefault((h, 7, 8 * vb + 7), []).append((ones_v, h + 1, vb))
                        hooks.setdefault((h, 7, 8 * vb + 7), []).append((prep_v, h + 1, vb))
                pending = []
                tcnt = {"n": 0}

                def fin_part(h, qg, c):
                    voff, ooff = vo(h)
                    s, ob = qg % 2, 6 + qg % 2
                    orow = slice(voff, voff + 64)
                    srow = slice(ooff, ooff + 64)
                    cc_ = slice(c * 128, (c + 1) * 128)
                    recip(rden[s][srow, cc_], ps[srow, ob, cc_], [PB[ob]], [RDN[s]])
                    tt("vector", tmpo[s][orow, cc_], ps[orow, ob, cc_], rden[s][srow, cc_], ALU.mult,
                       [PB[ob], RDN[s]], [TMO[s]])
                    dst = mixedT[orow, h // 2, qg * 512 + c * 128:qg * 512 + (c + 1) * 128]
                    tt("gpsimd", dst, tmpo[s][orow, cc_], dst, ALU.mult,
                       [TMO[s], MX[h // 2][qg]], [MX[h // 2][qg]])

                tiles = [(h, qg, kb) for h in range(8) for qg in range(8) for kb in range(64)]
                glist = []
                tpos = {"i": 0}
                last_prep = {"j": -100}

                def prep_would_fire(groups):
                    pq = [it_[0] for it_ in pending]
                    n_ = tcnt["n"]
                    for g_ in groups:
                        for (h, qg, kb) in g_:
                            if kb == 63:
                                pq += [fin_part] * 4
                            for fn, hh, idx in hooks.get((h, qg, kb), ()):
                                pq.append(fn)
                            n_ += 1
                            if pq and n_ % 3 == 0:
                                if pq.pop(0) is not fin_part:
                                    return True
                    return False

                def form_group(j):
                    if tpos["i"] >= len(tiles):
                        return False
                    assert j == len(glist)
                    n_ = 3
                    if j % 2 == 1:
                        prev = [glist[x] for x in (j - 2, j - 1) if x >= 0]
                        if last_prep["j"] > j - 5 or prep_would_fire(prev):
                            n_ = 2
                    glist.append(tiles[tpos["i"]:tpos["i"] + n_])
                    tpos["i"] += n_
                    return True

                def emit_qk(j):
                    sb0 = 0 if j % 2 == 0 else 3
                    for i, (h, qg, kb) in enumerate(glist[j]):
                        mm(ps[:, sb0 + i, :], KTm[0:96, kb * 128:(kb + 1) * 128],
                           QTm[0:96, qg * 512:(qg + 1) * 512], True, True,
                           [KNB[kb // 4], KRB[kb // 4], QMB[qg]], [PB[sb0 + i]])

                def emit_exp(j):
                    sb0 = 0 if j % 2 == 0 else 3
                    pslot = j % 3
                    ng = len(glist[j])
                    act(PT[pslot][:, 0:ng, :], ps[:, sb0:sb0 + ng, :], AF.Exp,
                        [PB[sb0 + i] for i in range(ng)], [PTB[pslot]], scale=MLA_SCALE)

                def emit_pv(j):
                    pslot = j % 3
                    gt = glist[j]
                    for i, (h, qg, kb) in enumerate(gt):
                        voff, ooff = vo(h)
                        ob = 6 + qg % 2
                        mm(ps[:, ob, :], VAm[:, kb, :], PT[pslot][:, i, :], kb == 0, kb == 63,
                           [VMB[kb // 8], PTB[pslot]], [PB[ob]])
                        for fn, hh, idx in now_hooks.get((h, qg, kb), ()):
                            fn(hh, idx)
                        if kb == 63:
                            for c in range(4):
                                pending.append((fin_part, h, qg, c))
                        for fn, hh, idx in hooks.get((h, qg, kb), ()):
                            pending.append((fn, hh, idx, 5))
                        tcnt["n"] += 1
                        if pending and tcnt["n"] % 3 == 0:
                            it_ = pending.pop(0)
                            if it_[0] is not fin_part:
                                last_prep["j"] = j
                            it_[0](*it_[1:])

                form_group(0)
                emit_qk(0)
                form_group(1)
                emit_qk(1)
                j = 0
                while j < len(glist):
                    emit_exp(j)
                    if form_group(j + 2):
                        emit_qk(j + 2)
                    emit_pv(j)
                    j += 1
                while pending:
                    it_ = pending.pop(0)
                    it_[0](*it_[1:])
            P.enabled = True
            P.barrier()

        with ExitStack() as sd:
            alloc_staging(sd, "d", full=False)
            xstage, XS = stg["xstage"], stg["XS"]
            woutb = T(sd, "woutb", [128, 8, 1024], BF16)
            WOB = Buf("wob")
            lng_t = T(sd, "lng_t", [128, 1024], F32)
            lnb_t = T(sd, "lnb_t", [128, 1024], F32)
            LNC = Buf("lnc")
            NS = 4
            junk = T(sd, "junk", [128, 1024], BF16)
            JNK = Buf("junk")
            yb = [T(sd, f"yb{i}", [128, 1024], F32) for i in range(NS)]
            YB = [Buf(f"yb{i}") for i in range(NS)]
            st6 = [T(sd, f"st6_{i}", [128, 12], F32) for i in range(NS)]
            ST6 = [Buf(f"st6_{i}") for i in range(NS)]
            mv = [T(sd, f"mv{i}", [128, 2], F32) for i in range(NS)]
            MV = [Buf(f"mv{i}") for i in range(NS)]
            rstd = [T(sd, f"rstd{i}", [128, 1], F32) for i in range(NS)]
            RSD = [Buf(f"rsd{i}") for i in range(NS)]
            nmr = [T(sd, f"nmr{i}", [128, 1], F32) for i in range(NS)]
            NMR = [Buf(f"nmr{i}") for i in range(NS)]
            load_weight_chunks(wout, 1024, lambda c: woutb[:, c, :], 8, WOB)
            dma("sync", lng_t[:], lng.partition_broadcast(128), (), [LNC])
            dma("sync", lnb_t[:], lnb.partition_broadcast(128), (), [LNC])
            stores = []
            NT = NQ // 128

            def load_x(t):
                if t < NT:
                    s3 = t % 3
                    dma("sync", xstage[s3][:], xq[t * 128:(t + 1) * 128, :], (), [XS[s3]])

            def d_a(t):
                s, s3 = t % NS, t % 3
                for hh in range(2):
                    b = (t % 2) * 2 + hh
                    for k in range(8):
                        mm(ps[:, b, :], mixedT[:, k, t * 128:(t + 1) * 128], woutb[:, k, hh * 512:(hh + 1) * 512],
                           k == 0, k == 7, [MX[k][t // 4], WOB], [PB[b]])
                    stt(yb[s][:, hh * 512:(hh + 1) * 512], xstage[s3][:, hh * 512:(hh + 1) * 512], ALPHA,
                        ps[:, b, :], ALU.mult, ALU.add, [XS[s3], PB[b]], [YB[s]])
                act(junk[:], yb[s][:], AF.Identity, [YB[s]], [JNK, ST6[s]], accum_out=st6[s][:, 0:1])
                act(junk[:], yb[s][:], AF.Square, [YB[s]], [JNK, ST6[s]], accum_out=st6[s][:, 1:2])

            def d_a2(t):
                s = t % NS
                ts("vector", mv[s][:, 0:1], st6[s][:, 0:1], 1.0 / D_MODEL, None, ALU.mult, None, [ST6[s]], [MV[s]])
                tt("vector", st6[s][:, 2:3], mv[s][:, 0:1], mv[s][:, 0:1], ALU.mult, [MV[s]], [ST6[s]])
                stt(mv[s][:, 1:2], st6[s][:, 1:2], 1.0 / D_MODEL, st6[s][:, 2:3], ALU.mult, ALU.subtract,
                    [ST6[s]], [MV[s]])
                act(rstd[s][:], mv[s][:, 1:2], AF.Sqrt, [MV[s]], [RSD[s]], bias=1e-5)

            def d_b1(t):
                s = t % NS
                recip(rstd[s][:], rstd[s][:], [RSD[s]], [RSD[s]])
                ts("vector", nmr[s][:], mv[s][:, 0:1], rstd[s][:, 0:1], -1.0, ALU.mult, ALU.mult,
                   [MV[s], RSD[s]], [NMR[s]])
                act(yb[s][:], yb[s][:], AF.Identity, [YB[s], RSD[s], NMR[s]], [YB[s]],
                    scale=rstd[s][:, 0:1], bias=nmr[s][:, 0:1])

            def d_b2(t):
                s = t % NS
                tt("vector", yb[s][:], yb[s][:], lng_t[:], ALU.mult, [YB[s], LNC], [YB[s]])
                tt("gpsimd", yb[s][:], yb[s][:], lnb_t[:], ALU.add, [YB[s], LNC], [YB[s]])
                stores.append(dma("sync", y[t * 128:(t + 1) * 128, :], yb[s][:], [YB[s]], []))

            load_x(0)
            load_x(1)
            load_x(2)
            d_a(0)
            d_a2(0)
            d_a(1)
            d_a2(1)
            d_b1(0)
            for t in range(NT):
                load_x(t + 3)
                if t + 2 < NT:
                    d_a(t + 2)
                d_b2(t)
                if t + 1 < NT:
                    d_b1(t + 1)
                if t + 2 < NT:
                    d_a2(t + 2)
            P.op("sync", None, deps=stores)
            P.op("scalar", None, deps=stores)

            with ExitStack() as se:
                sems = {e: se.enter_context(nc.semaphore(f"s_{e}")) for e in Prog.ENGS}
                dma_sems = {e: [se.enter_context(nc.semaphore(f"d_{e}{i}")) for i in range(Prog.DMA_POOL)]
                            for e in Prog.ENGS}
                P.finalize_tokens(sems, dma_sems)
                block = se.enter_context(nc.Block())

                @block.sync
                def _(e):
                    P.emit("sync", e)

                @block.tensor
                def _(e):
                    P.emit("tensor", e)

                @block.vector
                def _(e):
                    P.emit("vector", e)

                @block.scalar
                def _(e):
                    P.emit("scalar", e)

                @block.gpsimd
                def _(e):
                    P.emit("gpsimd", e)
    return nc


_CACHE = {}


def kernel(x, w_in, g_q, g_kv, w_uq, w_ukv, sink, rel_bias, w_out, ln_g, ln_b):
    x = np.asarray(x, np.float32)
    w_in = np.asarray(w_in, np.float32)
    w_uq = np.asarray(w_uq, np.float32)
    w_ukv = np.asarray(w_ukv, np.float32)
    w_out = np.asarray(w_out, np.float32)
    rel_bias = np.asarray(rel_bias, np.float32)
    o_cq, o_ckv, o_kr, o_ga, o_qs, o_ks, o_vs, o_gb = 0, 256, 384, 416, 928, 1440, 1568, 1696
    kr_cols = np.arange(o_kr, o_kr + 32)
    kr_sw = np.concatenate([kr_cols[16:], kr_cols[:16]])
    colsA = np.concatenate([np.arange(o_cq, o_cq + 256), np.arange(o_ckv, o_ckv + 128), kr_cols, kr_sw,
                            kr_cols, kr_sw, np.arange(o_ga, o_ga + 512)])
    k0 = np.arange(o_ks, o_ks + 64)
    k1 = np.arange(o_ks + 64, o_ks + 128)
    colsS = np.concatenate([np.arange(o_qs, o_qs + 512), k0, k0, k1, k1, np.arange(o_vs, o_vs + 128),
                            np.arange(o_gb, o_gb + 512)])
    wA = np.ascontiguousarray(w_in[:, colsA])
    wS = np.ascontiguousarray(w_in[:, colsS])
    cq = []
    for h in range(8):
        base = h * 96
        rope = np.arange(base + 64, base + 96)
        cq.append(np.concatenate([np.arange(base, base + 64), rope, rope[16:], rope[:16]]))
    wuq = np.ascontiguousarray(w_uq[:, np.concatenate(cq)])
    gq = np.ascontiguousarray(np.asarray(g_q, np.float32).reshape(2, 128).T)
    gkv = np.ascontiguousarray(np.asarray(g_kv, np.float32).reshape(128, 1))
    q_loc = np.arange(128)
    k_loc = np.arange(384) - 128
    rel = k_loc[None, :] - q_loc[:, None]
    band = np.abs(rel) <= 128
    bucket = _t5_bucket(rel)
    bias = rel_bias[bucket]
    bias = np.where(band[:, :, None], bias, np.float32(NEG)).astype(np.float32)
    bT = bias.reshape(128, 3, 128, 2, 4).transpose(2, 3, 1, 4, 0)
    bT = bT[:, :, :, [0, 2, 1, 3], :]
    biasT = np.ascontiguousarray(bT.reshape(128, 3072))
    ident = np.eye(128, dtype=np.float32)

    in_maps = []
    for c in range(N_CORES):
        b, half = c // 2, c % 2
        own0 = half * NQ
        oth0 = (1 - half) * NQ
        xq_c = x[b, own0:own0 + NQ]
        xo_c = x[b, oth0:oth0 + NQ]
        xh_c = np.zeros((256, D_MODEL), np.float32)
        hv_c = np.zeros((128, 2), np.float32)
        if own0 - 128 >= 0:
            xh_c[0:128] = x[b, own0 - 128:own0]
            hv_c[:, 0] = 1.0
        if own0 + NQ + 128 <= SEQ:
            xh_c[128:256] = x[b, own0 + NQ:own0 + NQ + 128]
            hv_c[:, 1] = 1.0
        pos = np.concatenate([np.arange(own0, own0 + NQ), np.arange(oth0, oth0 + NQ)])
        in_maps.append({
            "xq": np.ascontiguousarray(xq_c), "xo": np.ascontiguousarray(xo_c), "xh": xh_c, "hv": hv_c,
            "wukv": w_ukv, "gq": gq, "gkv": gkv,
            "cs": _rope_tables(pos), "sink": np.asarray(sink, np.float32),
            **{f"wA{i}": wA[i * 128:(i + 1) * 128] for i in range(8)},
            **{f"wS{i}": wS[i * 128:(i + 1) * 128] for i in range(8)},
            **{f"wuq{i}": wuq[i * 128:(i + 1) * 128] for i in range(2)},
            **{f"wout{i}": w_out[i * 128:(i + 1) * 128] for i in range(8)},
            **{f"biasT{kv}": np.ascontiguousarray(biasT[:, kv * 1536:(kv + 1) * 1536]) for kv in range(2)},
            "lng": np.asarray(ln_g, np.float32), "lnb": np.asarray(ln_b, np.float32), "ident": ident,
        })
    if "nc" not in _CACHE:
        _CACHE["nc"] = build_program()
    res = run_bass_kernel_spmd(_CACHE["nc"], in_maps, core_ids=list(range(N_CORES)))
    out = np.empty((BATCH, SEQ, D_MODEL), np.float32)
    for c in range(N_CORES):
        b, half = c // 2, c % 2
        out[b, half * NQ:(half + 1) * NQ] = res.results[c]["y"]
    return out
```
